# Optimizing a Trainium2 kernel written in Bass

```python
import math
import jax, jax.numpy as jnp
from jax import lax
import numpy as np

D_MODEL = 1024
BATCH = 2
SEQ = 8192
DEPTH = 2

CHUNK = 64
Q_BLOCK = 128
N_MIXERS = 2
FOX_HEADS = 16
FOX_HEAD_DIM = D_MODEL // FOX_HEADS
HGRN_EXPAND = 128
HGRN_HEADS = D_MODEL // HGRN_EXPAND
HGRN_HEAD_DIM = HGRN_EXPAND
D_FF = -(-8 * D_MODEL // (3 * 256)) * 256
N_FOX = (DEPTH + 1) // 2
N_HGRN = DEPTH // 2
EPS = 1e-6

kernel_name = 'hybrid_fox_hgrn2_trunk'


def rmsnorm(x, g):
    x32 = x.astype(jnp.float32)
    y = x32 * lax.rsqrt(jnp.mean(x32 * x32, axis=-1, keepdims=True) + EPS)
    return (y * g.astype(jnp.float32)).astype(x.dtype)


def fox_attention(h, w_in, b_f, w_out):
    bsz, seq, _ = h.shape
    proj = jnp.einsum('bsd,de->bse', h, w_in)
    q, k, v, f_logit = jnp.split(proj, [D_MODEL, 2 * D_MODEL, 3 * D_MODEL], axis=-1)

    def heads(t):
        return t.reshape(bsz, seq, FOX_HEADS, FOX_HEAD_DIM).transpose(0, 2, 1, 3)

    q = heads(q) * (FOX_HEAD_DIM ** -0.5)
    k = heads(k)
    v = heads(v)
    log_f = jax.nn.log_sigmoid(f_logit.astype(jnp.float32) + b_f.astype(jnp.float32))
    c = jnp.cumsum(log_f, axis=1).transpose(0, 2, 1)
    key_pos = jnp.arange(seq)

    def q_block(blk):
        start = blk * Q_BLOCK
        qb = lax.dynamic_slice_in_dim(q, start, Q_BLOCK, axis=2)
        cb = lax.dynamic_slice_in_dim(c, start, Q_BLOCK, axis=2)
        q_pos = start + jnp.arange(Q_BLOCK)
        logits = jnp.einsum('bhqd,bhkd->bhqk', qb, k).astype(jnp.float32)
        logits = logits + cb[:, :, :, None] - c[:, :, None, :]
        logits = jnp.where(q_pos[:, None] >= key_pos[None, :], logits, -jnp.inf)
        p = jax.nn.softmax(logits, axis=-1).astype(v.dtype)
        return jnp.einsum('bhqk,bhkd->bhqd', p, v)

    out = lax.map(q_block, jnp.arange(seq // Q_BLOCK))
    out = out.transpose(1, 0, 3, 2, 4).reshape(bsz, seq, D_MODEL)
    return jnp.einsum('bsd,de->bse', out, w_out)


def hgrn2_recurrence(h, w_in, lb, g_norm_w, w_out):
    bsz, seq, _ = h.shape
    n_chunks = seq // CHUNK
    proj = jnp.einsum('bsd,de->bse', h, w_in)
    q, f_logit, i_val, g = jnp.split(proj, [D_MODEL, 2 * D_MODEL, 3 * D_MODEL], axis=-1)
    f = lb + (1.0 - lb) * jax.nn.sigmoid(f_logit.astype(jnp.float32))
    log_f = jnp.log(f)
    k = 1.0 - f

    def to_chunks(t):
        t = t.astype(jnp.float32).reshape(bsz, n_chunks, CHUNK, HGRN_HEADS, HGRN_HEAD_DIM)
        return t.transpose(1, 0, 3, 2, 4)

    qc = to_chunks(q)
    kc = to_chunks(k)
    vc = to_chunks(i_val)
    bc = jnp.cumsum(to_chunks(log_f), axis=3)
    causal = jnp.arange(CHUNK)[:, None] >= jnp.arange(CHUNK)[None, :]

    def step(state, xs):
        q_t, k_t, v_t, b_t = xs
        o_inter = jnp.einsum('bhtk,bhkv->bhtv', q_t * jnp.exp(b_t), state)
        diff = b_t[:, :, :, None, :] - b_t[:, :, None, :, :]
        decay = jnp.exp(jnp.where(causal[:, :, None], diff, -jnp.inf))
        scores = jnp.einsum('bhtk,bhsk,bhtsk->bhts', q_t, k_t, decay)
        o = o_inter + jnp.einsum('bhts,bhsv->bhtv', scores, v_t)
        b_last = b_t[:, :, -1, :]
        k_dec = k_t * jnp.exp(b_last[:, :, None, :] - b_t)
        new_state = jnp.exp(b_last)[..., None] * state + jnp.einsum('bhsk,bhsv->bhkv', k_dec, v_t)
        return new_state, o

    state0 = jnp.zeros((bsz, HGRN_HEADS, HGRN_HEAD_DIM, HGRN_HEAD_DIM), jnp.float32)
    _, o = lax.scan(step, state0, (qc, kc, vc, bc))
    o = o.transpose(1, 0, 3, 2, 4).reshape(bsz, seq, HGRN_HEADS, HGRN_HEAD_DIM)
    g = g.astype(jnp.float32).reshape(bsz, seq, HGRN_HEADS, HGRN_HEAD_DIM)
    o = rmsnorm(o, g_norm_w) * jax.nn.silu(g)
    return jnp.einsum('bsd,de->bse', o.reshape(bsz, seq, D_MODEL).astype(h.dtype), w_out)


def swiglu_ffn(h, w_in, w_out):
    gate, up = jnp.split(jnp.einsum('bsd,df->bsf', h, w_in), 2, axis=-1)
    return jnp.einsum('bsf,fd->bsd', jax.nn.silu(gate) * up, w_out)


def setup_inputs(seed: int = 0) -> dict:
    key = jax.random.key(seed)
    ks = jax.random.split(key, 11)
    f32 = jnp.float32

    def normal(k, shape, fan_in):
        return jax.random.normal(k, shape, f32) * (fan_in ** -0.5)

    x = jax.random.normal(ks[0], (BATCH, SEQ, D_MODEL), f32)
    fox_w_in = normal(ks[1], (N_FOX, D_MODEL, 3 * D_MODEL + FOX_HEADS), D_MODEL)
    fox_b_f = jax.random.uniform(ks[2], (N_FOX, FOX_HEADS), f32, 1.0, 4.0)
    fox_w_out = normal(ks[3], (N_FOX, D_MODEL, D_MODEL), D_MODEL)
    hgrn_w_in = normal(ks[4], (N_HGRN, D_MODEL, 4 * D_MODEL), D_MODEL)
    hgrn_lb_table = 0.5 * jax.random.normal(ks[5], (DEPTH + 1, D_MODEL), f32)
    hgrn_gnorm = 1.0 + 0.02 * jax.random.normal(ks[6], (N_HGRN, HGRN_HEAD_DIM), f32)
    hgrn_w_out = normal(ks[7], (N_HGRN, D_MODEL, D_MODEL), D_MODEL)
    ffn_w_in = normal(ks[8], (DEPTH, D_MODEL, 2 * D_FF), D_MODEL)
    ffn_w_out = normal(ks[9], (DEPTH, D_FF, D_MODEL), D_FF)
    norm_gains = 1.0 + 0.02 * jax.random.normal(ks[10], (DEPTH, 4, D_MODEL), f32)
    return {'x': x, 'fox_w_in': fox_w_in, 'fox_b_f': fox_b_f, 'fox_w_out': fox_w_out,
            'hgrn_w_in': hgrn_w_in, 'hgrn_lb_table': hgrn_lb_table, 'hgrn_gnorm': hgrn_gnorm,
            'hgrn_w_out': hgrn_w_out, 'ffn_w_in': ffn_w_in, 'ffn_w_out': ffn_w_out,
            'norm_gains': norm_gains}


def reference(x, fox_w_in, fox_b_f, fox_w_out, hgrn_w_in, hgrn_lb_table, hgrn_gnorm,
              hgrn_w_out, ffn_w_in, ffn_w_out, norm_gains):
    lower_bounds = jnp.cumsum(jax.nn.softmax(hgrn_lb_table.astype(jnp.float32), axis=0), axis=0)
    h = x
    for layer in range(DEPTH):
        j = layer // N_MIXERS
        hn = rmsnorm(h, norm_gains[layer, 0])
        if layer % N_MIXERS == 0:
            mixed = fox_attention(hn, fox_w_in[j], fox_b_f[j], fox_w_out[j])
        else:
            mixed = hgrn2_recurrence(hn, hgrn_w_in[j], lower_bounds[layer], hgrn_gnorm[j],
                                     hgrn_w_out[j])
        h = h + rmsnorm(mixed, norm_gains[layer, 1])
        hn = rmsnorm(h, norm_gains[layer, 2])
        h = h + rmsnorm(swiglu_ffn(hn, ffn_w_in[layer], ffn_w_out[layer]), norm_gains[layer, 3])
    return h
```

```python
import numpy as np
import ml_dtypes
from contextlib import ExitStack
from concourse.bass_utils import run_bass_kernel_spmd
import concourse.bass as bass
import concourse.mybir as mybir

F32 = mybir.dt.float32
BF16 = mybir.dt.bfloat16
AF = mybir.ActivationFunctionType
ALU = mybir.AluOpType

BLOCKNAME = {'pe': 'tensor', 'act': 'scalar', 'dve': 'vector', 'pool': 'gpsimd', 'sp': 'sync'}
ENGS = ['pe', 'act', 'dve', 'pool', 'sp']
NDS = 8


class Buf:
    __slots__ = ('name', 'w', 'rs')

    def __init__(self, name=''):
        self.name = name
        self.w = None
        self.rs = []


class Op:
    __slots__ = ('eng', 'fn', 'deps', 'marked', 'semval', 'dma', 'dsem', 'dval', 'emitted')

    def __init__(self, eng, fn, dma=False):
        self.eng = eng
        self.fn = fn
        self.deps = []
        self.marked = False
        self.semval = None
        self.dma = dma
        self.dsem = None
        self.dval = None
        self.emitted = False


class Prog:
    def __init__(self, nc, stack):
        self.nc = nc
        self.sem = {e: stack.enter_context(nc.semaphore('sem_' + e)) for e in ENGS}
        self.dsems = {e: [stack.enter_context(nc.semaphore('dsem_%s_%d' % (e, k))) for k in range(NDS)]
                      for e in ENGS}
        self.dcount = {e: [0] * NDS for e in ENGS}
        self.drr = {e: 0 for e in ENGS}
        self.count = {e: 0 for e in ENGS}
        self.waited = {e: {} for e in ENGS}
        self.ops = {e: [] for e in ENGS}
        self.nops = 0

    def _add(self, op, reads, writes):
        deps = {}
        for b in reads:
            if b.w is not None:
                deps[id(b.w)] = (b.w, 'raw')
        for b in writes:
            if b.w is not None:
                deps[id(b.w)] = (b.w, 'waw')
            for r in b.rs:
                if id(r) not in deps:
                    deps[id(r)] = (r, 'war')
        for p, kind in deps.values():
            if p is op:
                continue
            if not p.dma and not op.dma and p.eng == op.eng:
                if op.eng == 'pe':
                    continue
                if kind == 'war':
                    continue
            op.deps.append(p)
            if not p.dma:
                p.marked = True
        for b in reads:
            b.rs.append(op)
        for b in writes:
            b.w = op
            b.rs = []
        self.ops[op.eng].append(op)
        self.nops += 1
        return op

    def op(self, eng, fn, reads=(), writes=()):
        return self._add(Op(eng, fn), list(reads), list(writes))

    def dma(self, q, out, in_, reads=(), writes=(), **kw):
        def fn(e):
            return e.dma_start(out=out, in_=in_, **kw)
        return self._add(Op(q, fn, dma=True), list(reads), list(writes))

    def _wait(self, engobj, e, sem, key, val):
        w = self.waited[e]
        if w.get(key, 0) >= val:
            return
        engobj.wait_ge(sem, val)
        w[key] = val

    def _emit_op(self, engobj, e, op):
        for p in op.deps:
            if p.dma:
                assert p.emitted or p.dval is not None, 'dma dep not assigned'
                self._wait(engobj, e, p.dsem, ('d', id(p.dsem)), p.dval)
            else:
                assert p.semval is not None, 'dep on unassigned op'
                self._wait(engobj, e, self.sem[p.eng], ('e', p.eng), p.semval)
        if op.dma:
            if op.dval > 16:
                self._wait(engobj, e, op.dsem, ('d', id(op.dsem)), op.dval - 16)
            ins = op.fn(engobj)
            ins.then_inc(op.dsem, 16)
        else:
            ins = op.fn(engobj)
            if op.marked:
                ins.then_inc(self.sem[e], 1)
        op.emitted = True

    def emit(self):
        for e in ENGS:
            for op in self.ops[e]:
                if op.dma:
                    k = self.drr[e]
                    self.drr[e] = (k + 1) % NDS
                    self.dcount[e][k] += 16
                    op.dsem = self.dsems[e][k]
                    op.dval = self.dcount[e][k]
                elif op.marked:
                    self.count[e] += 1
                    op.semval = self.count[e]
        with self.nc.Block() as block:
            for e in ENGS:
                ops = self.ops[e]
                if not ops:
                    continue

                def body(engobj, e=e, ops=ops):
                    for op in ops:
                        self._emit_op(engobj, e, op)
                    for k in range(NDS):
                        if self.dcount[e][k] > 0:
                            self._wait(engobj, e, self.dsems[e][k], ('d', id(self.dsems[e][k])),
                                       self.dcount[e][k])
                getattr(block, BLOCKNAME[e])(body)
        self.ops = {e: [] for e in ENGS}


NH = 4
HD = 64
D = 1024
KC = 8
EPS = 1e-6


def build_phase_a(nc, P, stack, S, xT, w, g0, bf, tri_in, attn_out):
    NT = S // 512
    NKB = S // 128
    sb = lambda name, shape, dt: stack.enter_context(nc.sbuf_tensor(name, shape, dt))
    ps = lambda name, shape, dt: stack.enter_context(nc.psum_tensor(name, shape, dt))

    ones_bf = sb('ones_bf', [128, 128], BF16)
    tri_f = sb('tri_f', [128, 128], F32)
    tri_bf = sb('tri_bf', [128, 128], BF16)
    sel = sb('sel', [65, 64], F32)
    g0_sb = sb('g0_sb', [128, 8], F32)
    bf_sb = sb('bf_sb', [4, 1], F32)
    negb = sb('negb', [4, 1], F32)
    eps_sb = sb('eps_sb', [128, 1], F32)
    one_sb = sb('one_sb', [128, 1], F32)
    ones4 = sb('ones4', [4, 512], F32)
    Wb = sb('Wb', [128, KC, 772], BF16)
    xf = [sb('xf%d' % i, [128, KC, 512], F32) for i in range(2)]
    sq = sb('sq', [128, KC, 512], BF16)
    xb = [sb('xb%d' % i, [128, KC, 512], BF16) for i in range(2)]
    lnr = sb('lnr', [128, 512], F32)
    rstd = sb('rstd', [128, 512], F32)
    qaug = [sb('qaug%d' % i, [70, NH, 512], BF16) for i in range(2)]
    kaug = sb('kaug', [70, NH, S], BF16)
    vaug = sb('vaug', [128, NKB, NH, 65], BF16)
    ef = sb('ef', [4, 512], F32)
    lf = sb('lf', [4, 512], F32)
    Lc = [sb('Lc%d' % i, [4, 512], F32) for i in range(2)]
    r1 = sb('r1', [4, 512], F32)
    r2 = sb('r2', [4, 512], F32)
    pc = [sb('pc%d' % i, [4, 3, 512], BF16) for i in range(2)]
    pT = [sb('pT%d' % i, [128, 512], BF16) for i in range(3)]
    o_sb = [sb('o_sb%d' % i, [65, 512], F32) for i in range(2)]
    a_sb = [sb('a_sb%d' % i, [64, 512], BF16) for i in range(2)]
    bank = [ps('bank%d' % i, [128, 512], F32) for i in range(8)]

    B = Buf
    b_ones, b_tri, b_sel, b_g0, b_bf, b_negb, b_eps, b_wst, b_Wb = [B() for _ in range(9)]
    b_xf = [B(), B()]
    b_sq = B()
    b_xb = [B(), B()]
    b_lnr, b_rstd = B(), B()
    b_qaug = [[B() for h in range(NH)] for i in range(2)]
    b_qaugL = [[B() for h in range(NH)] for i in range(2)]
    b_k = [[B() for h in range(NH)] for j in range(NT)]
    b_kL = [[B() for h in range(NH)] for j in range(NT)]
    b_v = [B() for j in range(NT)]
    b_ef, b_lf, b_r1, b_r2 = B(), B(), B(), B()
    b_Lc = [B() for j in range(NT)]
    b_pc = [B() for j in range(NT)]
    b_pT = [B() for i in range(3)]
    b_osb = [B(), B()]
    b_asb = [B(), B()]
    b_bank = [B() for i in range(8)]
    b_misc = B()

    P.op('pool', lambda e: e.memset(ones_bf[:], 1.0), writes=[b_ones])
    P.op('pool', lambda e: e.memset(eps_sb[:], EPS), writes=[b_eps])
    P.op('pool', lambda e: e.memset(one_sb[:], 1.0), writes=[b_eps])
    P.op('pool', lambda e: e.memset(ones4[:], 1.0), writes=[b_misc])
    P.op('pool', lambda e: e.memset(sel[:], 0.0), writes=[b_sel])
    P.op('pool', lambda e: e.memset(sel[64:65, :], 1.0), writes=[b_sel])
    P.op('pool', lambda e: e.memset(vaug[:], 1.0), writes=b_v)
    P.op('pool', lambda e: e.memset(kaug[64:70, :, :], -1.0), writes=[x for j in range(NT) for x in b_kL[j]])
    for i in range(2):
        P.op('pool', lambda e, i=i: e.memset(qaug[i][64:70, :, :], 1.0), writes=b_qaugL[i])
    P.dma('sp', tri_f[:], tri_in[:, :], writes=[b_tri])
    P.op('dve', lambda e: e.tensor_copy(out=tri_bf[:], in_=tri_f[:]), reads=[b_tri], writes=[b_tri])
    P.dma('sp', g0_sb[:], g0[:, :], writes=[b_g0])
    P.dma('sp', bf_sb[:], bf[:, :], writes=[b_bf])
    P.op('dve', lambda e: e.tensor_scalar(out=negb[:], in0=bf_sb[:], scalar1=-1.0, scalar2=None, op0=ALU.mult),
         reads=[b_bf], writes=[b_negb])
    wv = w.rearrange("(kc p) n -> p kc n", p=128)
    for kc in range(KC):
        P.dma('sp', xf[0][:, kc, :], wv[:, kc, 0:512], writes=[b_xf[0]])
        P.dma('sp', xf[1][:, kc, 0:260], wv[:, kc, 512:772], writes=[b_xf[1]])
    for kc in range(KC):
        P.op('dve', lambda e, kc=kc: e.tensor_scalar(out=Wb[:, kc, 0:256], in0=xf[0][:, kc, 0:256],
                                                     scalar1=g0_sb[:, kc:kc + 1], scalar2=0.125,
                                                     op0=ALU.mult, op1=ALU.mult),
             reads=[b_xf[0], b_g0], writes=[b_Wb])
        P.op('pool', lambda e, kc=kc: e.tensor_scalar(out=Wb[:, kc, 256:512], in0=xf[0][:, kc, 256:512],
                                                      scalar1=g0_sb[:, kc:kc + 1], scalar2=None,
                                                      op0=ALU.mult),
             reads=[b_xf[0], b_g0], writes=[b_Wb])
        P.op('pool', lambda e, kc=kc: e.tensor_scalar(out=Wb[:, kc, 512:772], in0=xf[1][:, kc, 0:260],
                                                      scalar1=g0_sb[:, kc:kc + 1], scalar2=None,
                                                      op0=ALU.mult),
             reads=[b_xf[1], b_g0], writes=[b_Wb])

    xv = xT.rearrange("(kc p) t -> p kc t", p=128)
    pb = [0]

    def next_pbank():
        k = pb[0]
        pb[0] = 1 - k
        return k

    def proj_stage(j):
        bi = j % 2
        tsl = slice(j * 512, (j + 1) * 512)
        for half in range(2):
            P.dma('sp', xf[bi][:, half * 4:(half + 1) * 4, :], xv[:, half * 4:(half + 1) * 4, tsl],
                  writes=[b_xf[bi]])
        P.op('act', lambda e: e.activation(out=sq[:], in_=xf[bi][:], func=AF.Square),
             reads=[b_xf[bi]], writes=[b_sq])
        k = next_pbank()
        for kc in range(KC):
            P.op('pe', lambda e, kc=kc, k=k: e.matmul(bank[k][:, :], lhsT=ones_bf[:], rhs=sq[:, kc, :],
                                                      start=(kc == 0), stop=(kc == KC - 1)),
                 reads=[b_ones, b_sq], writes=[b_bank[k]])
        P.op('act', lambda e, k=k: e.activation(out=lnr[:], in_=bank[k][:, :], func=AF.Ln,
                                                bias=eps_sb[:, 0:1], scale=1.0 / D),
             reads=[b_bank[k], b_eps], writes=[b_lnr])
        P.op('act', lambda e: e.activation(out=rstd[:], in_=lnr[:], func=AF.Exp, scale=-0.5),
             reads=[b_lnr], writes=[b_rstd])
        for kc in range(KC):
            eng = 'dve' if kc % 2 == 0 else 'pool'
            P.op(eng, lambda e, kc=kc: e.tensor_tensor(out=xb[bi][:, kc, :], in0=xf[bi][:, kc, :],
                                                       in1=rstd[:], op=ALU.mult),
                 reads=[b_xf[bi], b_rstd], writes=[b_xb[bi]])
        for h in range(NH):
            k = next_pbank()
            for kc in range(KC):
                P.op('pe', lambda e, kc=kc, k=k, h=h: e.matmul(
                    bank[k][0:64, :], lhsT=Wb[:, kc, h * 64:(h + 1) * 64], rhs=xb[bi][:, kc, :],
                    start=(kc == 0), stop=(kc == KC - 1)),
                    reads=[b_Wb, b_xb[bi]], writes=[b_bank[k]])
            P.op('act', lambda e, k=k, h=h: e.activation(out=qaug[bi][0:64, h, :], in_=bank[k][0:64, :],
                                                        func=AF.Copy),
                 reads=[b_bank[k]], writes=[b_qaug[bi][h]])
            k = next_pbank()
            for kc in range(KC):
                P.op('pe', lambda e, kc=kc, k=k, h=h: e.matmul(
                    bank[k][0:64, :], lhsT=Wb[:, kc, 256 + h * 64:256 + (h + 1) * 64], rhs=xb[bi][:, kc, :],
                    start=(kc == 0), stop=(kc == KC - 1)),
                    reads=[b_Wb, b_xb[bi]], writes=[b_bank[k]])
            P.op('dve', lambda e, k=k, h=h: e.tensor_copy(out=kaug[0:64, h, tsl], in_=bank[k][0:64, :]),
                 reads=[b_bank[k]], writes=[b_k[j][h]])
        for half in range(2):
            k = next_pbank()
            for s2 in range(2):
                sub = half * 2 + s2
                for kc in range(KC):
                    P.op('pe', lambda e, kc=kc, k=k, sub=sub, s2=s2: e.matmul(
                        bank[k][:, s2 * 256:(s2 + 1) * 256], lhsT=xb[bi][:, kc, sub * 128:(sub + 1) * 128],
                        rhs=Wb[:, kc, 512:768], start=(kc == 0), stop=(kc == KC - 1)),
                        reads=[b_Wb, b_xb[bi]], writes=[b_bank[k]])
            for s2 in range(2):
                sub = half * 2 + s2
                eng = 'act' if s2 == 0 else 'dve'
                if eng == 'act':
                    P.op('act', lambda e, k=k, sub=sub, s2=s2: e.activation(
                        out=vaug[:, 4 * j + sub, :, 0:64],
                        in_=bank[k][:, s2 * 256:(s2 + 1) * 256].rearrange("p (h d) -> p h d", h=NH),
                        func=AF.Copy), reads=[b_bank[k]], writes=[b_v[j]])
                else:
                    P.op('dve', lambda e, k=k, sub=sub, s2=s2: e.tensor_copy(
                        out=vaug[:, 4 * j + sub, :, 0:64],
                        in_=bank[k][:, s2 * 256:(s2 + 1) * 256].rearrange("p (h d) -> p h d", h=NH)),
                        reads=[b_bank[k]], writes=[b_v[j]])
        k = next_pbank()
        for kc in range(KC):
            P.op('pe', lambda e, kc=kc, k=k: e.matmul(bank[k][0:4, :], lhsT=Wb[:, kc, 768:772],
                                                      rhs=xb[bi][:, kc, :], start=(kc == 0),
                                                      stop=(kc == KC - 1)),
                 reads=[b_Wb, b_xb[bi]], writes=[b_bank[k]])
        P.op('act', lambda e, k=k: e.activation(out=ef[:], in_=bank[k][0:4, :], func=AF.Exp,
                                                bias=negb[:, 0:1], scale=-1.0),
             reads=[b_bank[k], b_negb], writes=[b_ef])
        P.op('act', lambda e: e.activation(out=lf[:], in_=ef[:], func=AF.Ln, bias=one_sb[0:4, 0:1], scale=1.0),
             reads=[b_ef, b_eps], writes=[b_lf])
        init = 0.0 if j == 0 else Lc[1 - bi][:, 511:512]
        rd = [b_lf, b_misc] + ([b_Lc[1 - bi]] if j > 0 else [])
        P.op('dve', lambda e: e.tensor_tensor_scan(out=Lc[bi][:, :], data0=ones4[:], data1=lf[:], initial=init,
                                                   op0=ALU.mult, op1=ALU.add),
             reads=rd, writes=[b_Lc[bi]])
        P.op('dve', lambda e: e.tensor_copy(out=pc[bi][:, 0, :], in_=Lc[bi][:, :]), reads=[b_Lc[bi]], writes=[b_pc[bi]])
        P.op('dve', lambda e: e.tensor_tensor(out=r1[:], in0=Lc[bi][:, :], in1=pc[bi][:, 0, :], op=ALU.subtract),
             reads=[b_Lc[bi], b_pc[bi]], writes=[b_r1])
        P.op('dve', lambda e: e.tensor_copy(out=pc[bi][:, 1, :], in_=r1[:]), reads=[b_r1], writes=[b_pc[bi]])
        P.op('dve', lambda e: e.tensor_tensor(out=r2[:], in0=r1[:], in1=pc[bi][:, 1, :], op=ALU.subtract),
             reads=[b_r1, b_pc[bi]], writes=[b_r2])
        P.op('dve', lambda e: e.tensor_copy(out=pc[bi][:, 2, :], in_=r2[:]), reads=[b_r2], writes=[b_pc[bi]])
        for h in range(NH):
            for i in range(3):
                P.dma('sp', qaug[bi][64 + i:65 + i, h, :], pc[bi][h:h + 1, i, :],
                      reads=[b_pc[bi]], writes=[b_qaugL[bi][h]])
                P.dma('sp', kaug[67 + i:68 + i, h, tsl], pc[bi][h:h + 1, i, :],
                      reads=[b_pc[bi]], writes=[b_kL[j][h]])

    stc = [0]

    def attn_stage(j):
        bi = j % 2
        tsl = slice(j * 512, (j + 1) * 512)
        blocks = []
        for h in range(NH):
            nb = 4 * j + 4
            for kb in range(nb):
                i = kb - 4 * j
                col0 = 128 * i if i > 0 else 0
                blocks.append((h, kb, col0, i >= 0, kb == 0, kb == nb - 1))
        nblk = len(blocks)
        sbank = {}

        def emit_S(n):
            h, kb, col0, diag, first, last = blocks[n]
            s = stc[0] % 3
            stc[0] += 1
            sbank[n] = s
            jj = kb // 4
            P.op('pe', lambda e: e.matmul(bank[3 + s][:, col0:512], lhsT=kaug[0:70, h, kb * 128:(kb + 1) * 128],
                                          rhs=qaug[bi][0:70, h, col0:512], start=True, stop=True),
                 reads=[b_k[jj][h], b_kL[jj][h], b_qaug[bi][h], b_qaugL[bi][h]], writes=[b_bank[3 + s]])
            P.op('act', lambda e: e.activation(out=pT[s][:, col0:512], in_=bank[3 + s][:, col0:512], func=AF.Exp),
                 reads=[b_bank[3 + s]], writes=[b_pT[s]])
            if diag:
                P.op('pool', lambda e: e.tensor_tensor(out=pT[s][:, col0:col0 + 128], in0=pT[s][:, col0:col0 + 128],
                                                       in1=tri_bf[:], op=ALU.mult),
                     reads=[b_pT[s], b_tri], writes=[b_pT[s]])

        def emit_PV(n):
            h, kb, col0, diag, first, last = blocks[n]
            s = sbank[n]
            ob = 6 + (h % 2)
            jj = kb // 4
            P.op('pe', lambda e: e.matmul(bank[ob][0:65, col0:512], lhsT=vaug[:, kb, h, :], rhs=pT[s][:, col0:512],
                                          start=first, stop=last),
                 reads=[b_v[jj], b_pT[s]], writes=[b_bank[ob]])
            if last:
                oi = h % 2
                P.op('act', lambda e: e.activation(out=o_sb[oi][:], in_=bank[ob][0:65, :], func=AF.Copy),
                     reads=[b_bank[ob]], writes=[b_osb[oi]])
                P.op('dve', lambda e: e.reciprocal(out=o_sb[oi][64:65, :], in_=o_sb[oi][64:65, :]),
                     reads=[b_osb[oi]], writes=[b_osb[oi]])
                P.op('pe', lambda e: e.matmul(bank[2][0:64, :], lhsT=sel[:], rhs=o_sb[oi][:], start=True, stop=True),
                     reads=[b_sel, b_osb[oi]], writes=[b_bank[2]])
                P.op('dve', lambda e: e.tensor_tensor(out=a_sb[oi][:], in0=o_sb[oi][0:64, :], in1=bank[2][0:64, :],
                                                      op=ALU.mult),
                     reads=[b_osb[oi], b_bank[2]], writes=[b_asb[oi]])
                P.dma('sp', attn_out[h * 64:(h + 1) * 64, tsl], a_sb[oi][:], reads=[b_asb[oi]])

        emit_S(0)
        if nblk > 1:
            emit_S(1)
        for n in range(nblk):
            if n + 2 < nblk:
                emit_S(n + 2)
            emit_PV(n)

    for j in range(NT):
        proj_stage(j)
        attn_stage(j)


DFF = 2816
FC = 22


def build_phase_d(nc, P, stack, T, aT, resT, w_o, w_fi, w_fo, gains, outT):
    NTT = T // 512
    sb = lambda name, shape, dt: stack.enter_context(nc.sbuf_tensor(name, shape, dt))
    ps = lambda name, shape, dt: stack.enter_context(nc.psum_tensor(name, shape, dt))
    B = Buf
    ones_bf = sb('ones_bf', [128, 128], BF16)
    eps_sb = sb('eps_sb', [128, 1], F32)
    g_sb = sb('g_sb', [128, 3, 8], F32)
    a_sb = sb('a_sb', [128, KC, 512], BF16)
    r_sb = sb('r_sb', [128, KC, 512], F32)
    mixed = sb('mixed', [128, KC, 512], F32)
    h_sb = sb('h_sb', [128, KC, 512], F32)
    hn = sb('hn', [128, KC, 512], BF16)
    sq = sb('sq', [128, KC, 512], BF16)
    lnr = sb('lnr', [128, 512], F32)
    rstd = sb('rstd', [128, 512], F32)
    tmp = [sb('tmp%d' % i, [128, 512], F32) for i in range(2)]
    sg = [sb('sg%d' % i, [128, 512], F32) for i in range(2)]
    act = sb('act', [128, FC, 512], BF16)
    wst = [sb('wst%d' % i, [128, FC, 256], F32) for i in range(2)]
    wb = [sb('wb%d' % i, [128, FC, 256], BF16) for i in range(3)]
    bank = [ps('bank%d' % i, [128, 512], F32) for i in range(8)]
    b_ones, b_eps, b_g, b_a, b_r, b_mixed, b_h, b_hn, b_sq, b_lnr, b_rstd, b_act = [B() for _ in range(12)]
    b_tmp = [B(), B()]
    b_sg = [B(), B()]
    b_wst = [B(), B()]
    b_wb = [B(), B(), B()]
    b_bank = [B() for _ in range(8)]

    P.op('pool', lambda e: e.memset(ones_bf[:], 1.0), writes=[b_ones])
    P.op('pool', lambda e: e.memset(eps_sb[:], EPS), writes=[b_eps])
    P.dma('sp', g_sb[:], gains[:, :, :], writes=[b_g])

    wcnt = [0]
    wbc = [0]

    def stream_w(wd, nk, col0, ncols, gi):
        si = wcnt[0] % 2
        wcnt[0] += 1
        bi = wbc[0] % 3
        wbc[0] += 1
        wv = wd.rearrange("(kc p) n -> p kc n", p=128)
        step = 8
        for k0 in range(0, nk, step):
            k1 = min(nk, k0 + step)
            P.dma('sp', wst[si][:, k0:k1, 0:ncols], wv[:, k0:k1, col0:col0 + ncols], writes=[b_wst[si]])
        if gi is None:
            half = nk // 2
            P.op('dve', lambda e: e.tensor_copy(out=wb[bi][:, 0:half, 0:ncols], in_=wst[si][:, 0:half, 0:ncols]),
                 reads=[b_wst[si]], writes=[b_wb[bi]])
            P.op('pool', lambda e: e.tensor_copy(out=wb[bi][:, half:nk, 0:ncols], in_=wst[si][:, half:nk, 0:ncols]),
                 reads=[b_wst[si]], writes=[b_wb[bi]])
        else:
            for kc in range(nk):
                eng = 'dve' if kc % 2 == 0 else 'pool'
                P.op(eng, lambda e, kc=kc: e.tensor_scalar(out=wb[bi][:, kc, 0:ncols], in0=wst[si][:, kc, 0:ncols],
                                                           scalar1=g_sb[:, gi, kc:kc + 1], scalar2=None,
                                                           op0=ALU.mult),
                     reads=[b_wst[si], b_g], writes=[b_wb[bi]])
        return wb[bi], b_wb[bi]

    pbk = [0]

    def nbank():
        k = pbk[0] % 6
        pbk[0] += 1
        return k

    def rms_stats(src_buf):
        for kc in range(KC):
            P.op('pe', lambda e, kc=kc: e.matmul(bank[7][:, :], lhsT=ones_bf[:], rhs=sq[:, kc, :],
                                                 start=(kc == 0), stop=(kc == KC - 1)),
                 reads=[b_ones, b_sq], writes=[b_bank[7]])
        P.op('act', lambda e: e.activation(out=lnr[:], in_=bank[7][:, :], func=AF.Ln, bias=eps_sb[:, 0:1],
                                           scale=1.0 / D), reads=[b_bank[7], b_eps], writes=[b_lnr])
        P.op('act', lambda e: e.activation(out=rstd[:], in_=lnr[:], func=AF.Exp, scale=-0.5),
             reads=[b_lnr], writes=[b_rstd])

    def post_norm_residual(y, b_y, gi, res, b_res, out, b_out):
        for mc in range(KC):
            ti = mc % 2
            P.op('pool', lambda e, mc=mc, ti=ti: e.tensor_tensor(out=tmp[ti][:], in0=y[:, mc, :], in1=rstd[:],
                                                                 op=ALU.mult),
                 reads=[b_y, b_rstd], writes=[b_tmp[ti]])
            P.op('dve', lambda e, mc=mc, ti=ti: e.scalar_tensor_tensor(out=out[:, mc, :], in0=tmp[ti][:],
                                                                       scalar=g_sb[:, gi, mc:mc + 1],
                                                                       in1=res[:, mc, :], op0=ALU.mult, op1=ALU.add),
                 reads=[b_tmp[ti], b_g, b_res], writes=[b_out])

    aV = aT.rearrange("(kc p) t -> p kc t", p=128)
    rV = resT.rearrange("(kc p) t -> p kc t", p=128)
    oV = outT.rearrange("(kc p) t -> p kc t", p=128)

    for t in range(NTT):
        tsl = slice(t * 512, (t + 1) * 512)
        for half in range(2):
            P.dma('sp', a_sb[:, half * 4:(half + 1) * 4, :], aV[:, half * 4:(half + 1) * 4, tsl], writes=[b_a])
            P.dma('sp', r_sb[:, half * 4:(half + 1) * 4, :], rV[:, half * 4:(half + 1) * 4, tsl], writes=[b_r])
        for cb in range(4):
            wt, bw = stream_w(w_o, KC, cb * 256, 256, None)
            for m2 in range(2):
                mc = cb * 2 + m2
                k = nbank()
                for kc in range(KC):
                    P.op('pe', lambda e, kc=kc, k=k, m2=m2, wt=wt: e.matmul(
                        bank[k][:, :], lhsT=wt[:, kc, m2 * 128:(m2 + 1) * 128], rhs=a_sb[:, kc, :],
                        start=(kc == 0), stop=(kc == KC - 1)), reads=[bw, b_a], writes=[b_bank[k]])
                P.op('act', lambda e, k=k, mc=mc: e.activation(out=mixed[:, mc, :], in_=bank[k][:, :], func=AF.Copy),
                     reads=[b_bank[k]], writes=[b_mixed])
                P.op('act', lambda e, k=k, mc=mc: e.activation(out=sq[:, mc, :], in_=bank[k][:, :], func=AF.Square),
                     reads=[b_bank[k]], writes=[b_sq])
        rms_stats(None)
        post_norm_residual(mixed, b_mixed, 0, r_sb, b_r, h_sb, b_h)
        P.op('act', lambda e: e.activation(out=sq[:], in_=h_sb[:], func=AF.Square), reads=[b_h], writes=[b_sq])
        rms_stats(None)
        for kc in range(KC):
            eng = 'dve' if kc % 2 == 0 else 'pool'
            P.op(eng, lambda e, kc=kc: e.tensor_tensor(out=hn[:, kc, :], in0=h_sb[:, kc, :], in1=rstd[:], op=ALU.mult),
                 reads=[b_h, b_rstd], writes=[b_hn])
        for cb in range(11):
            wg, bwg = stream_w(w_fi, KC, cb * 256, 256, 1)
            wu, bwu = stream_w(w_fi, KC, DFF + cb * 256, 256, 1)
            for m2 in range(2):
                fc = cb * 2 + m2
                kg = nbank()
                for kc in range(KC):
                    P.op('pe', lambda e, kc=kc, k=kg, m2=m2, wt=wg: e.matmul(
                        bank[k][:, :], lhsT=wt[:, kc, m2 * 128:(m2 + 1) * 128], rhs=hn[:, kc, :],
                        start=(kc == 0), stop=(kc == KC - 1)), reads=[bwg, b_hn], writes=[b_bank[kg]])
                si = fc % 2
                P.op('act', lambda e, k=kg, si=si: e.activation(out=sg[si][:], in_=bank[k][:, :], func=AF.Silu),
                     reads=[b_bank[kg]], writes=[b_sg[si]])
                ku = nbank()
                for kc in range(KC):
                    P.op('pe', lambda e, kc=kc, k=ku, m2=m2, wt=wu: e.matmul(
                        bank[k][:, :], lhsT=wt[:, kc, m2 * 128:(m2 + 1) * 128], rhs=hn[:, kc, :],
                        start=(kc == 0), stop=(kc == KC - 1)), reads=[bwu, b_hn], writes=[b_bank[ku]])
                P.op('dve', lambda e, k=ku, si=si, fc=fc: e.tensor_tensor(out=act[:, fc, :], in0=bank[k][:, :],
                                                                          in1=sg[si][:], op=ALU.mult),
                     reads=[b_bank[ku], b_sg[si]], writes=[b_act])
        for cb in range(4):
            wt, bw = stream_w(w_fo, FC, cb * 256, 256, None)
            for m2 in range(2):
                mc = cb * 2 + m2
                k = nbank()
                for kc in range(FC):
                    P.op('pe', lambda e, kc=kc, k=k, m2=m2, wt=wt: e.matmul(
                        bank[k][:, :], lhsT=wt[:, kc, m2 * 128:(m2 + 1) * 128], rhs=act[:, kc, :],
                        start=(kc == 0), stop=(kc == FC - 1)), reads=[bw, b_act], writes=[b_bank[k]])
                P.op('act', lambda e, k=k, mc=mc: e.activation(out=mixed[:, mc, :], in_=bank[k][:, :], func=AF.Copy),
                     reads=[b_bank[k]], writes=[b_mixed])
                P.op('act', lambda e, k=k, mc=mc: e.activation(out=sq[:, mc, :], in_=bank[k][:, :], func=AF.Square),
                     reads=[b_bank[k]], writes=[b_sq])
        rms_stats(None)
        post_norm_residual(mixed, b_mixed, 2, h_sb, b_h, r_sb, b_r)
        for half in range(2):
            P.dma('sp', oV[:, half * 4:(half + 1) * 4, tsl], r_sb[:, half * 4:(half + 1) * 4, :], reads=[b_r])


def build_phase_r(nc, P, stack, S, hT, w, g4, lbt, gnorm, mask2_in, rmask_in, ident_in, og):
    NT = S // 512
    sb = lambda name, shape, dt: stack.enter_context(nc.sbuf_tensor(name, shape, dt))
    ps = lambda name, shape, dt: stack.enter_context(nc.psum_tensor(name, shape, dt))
    B = Buf
    ones_bf = sb('ones_bf', [128, 128], BF16)
    eps_sb = sb('eps_sb', [128, 1], F32)
    g4_sb = sb('g4_sb', [128, 8], F32)
    gn_sb = sb('gn_sb', [128, 1], F32)
    lbt_sb = sb('lbt_sb', [128, 3, 2], F32)
    elb = sb('elb', [128, 3, 2], F32)
    s01 = sb('s01', [128, 2], F32)
    sall = sb('sall', [128, 2], F32)
    rs = sb('rs', [128, 2], F32)
    lb = sb('lb', [128, 2], F32)
    oml = sb('oml', [128, 2], F32)
    noml = sb('noml', [128, 2], F32)
    mask2_f = sb('mask2_f', [128, 128], F32)
    rmask = sb('rmask_sb', [128, 512], F32)
    ident_f = sb('ident_f', [128, 128], F32)
    ident = sb('ident_bf', [128, 128], BF16)
    wst = sb('wst', [128, KC, 512], F32)
    Wb = sb('Wb', [128, KC, 1024], BF16)
    sq = sb('sq', [128, KC, 512], BF16)
    hn = sb('hn', [128, KC, 512], BF16)
    lnr = sb('lnr', [128, 512], F32)
    rstd = sb('rstd', [128, 512], F32)
    q_sb = [sb('q_sb%d' % i, [128, 512], F32) for i in range(2)]
    sig = [sb('sig%d' % i, [128, 512], F32) for i in range(2)]
    logf = [sb('logf%d' % i, [128, 512], F32) for i in range(2)]
    kk = [sb('kk%d' % i, [128, 512], F32) for i in range(2)]
    bb = [sb('bb%d' % i, [128, 512], F32) for i in range(2)]
    eb = [sb('eb%d' % i, [128, 512], F32) for i in range(2)]
    enb = [sb('enb%d' % i, [128, 512], F32) for i in range(2)]
    qt = [sb('qt%d' % i, [128, 512], BF16) for i in range(2)]
    kt = [sb('kt%d' % i, [128, 512], BF16) for i in range(2)]
    sgl = [sb('sgl%d' % i, [128, 512], F32) for i in range(2)]
    v_tok = sb('v_tok', [128, 4, 256], BF16)
    kt_tok = sb('kt_tok', [128, 2, 4, 128], BF16)
    at_sb = [sb('at_sb%d' % i, [128, 128], BF16) for i in range(2)]
    T_sb = [sb('T_sb%d' % i, [128, 128], F32) for i in range(2)]
    St = [sb('St%d' % i, [128, 128], F32) for i in range(2)]
    Sbf = [[sb('Sbf%d_%d' % (i, v), [128, 128], BF16) for v in range(2)] for i in range(2)]
    o_sb = [sb('o_sb%d' % i, [128, 512], F32) for i in range(2)]
    osq = sb('osq', [128, 512], BF16)
    tmp = sb('tmp', [128, 512], F32)
    og_sb = [sb('og_sb%d' % i, [128, 512], BF16) for i in range(2)]
    bank = [ps('bank%d' % i, [128, 512], F32) for i in range(7)]
    bankT = ps('bankT', [128, 1024], BF16)

    b_const, b_g4, b_lb, b_mask2, b_rmask, b_ident, b_wst, b_Wb, b_sq, b_hn, b_lnr, b_rstd = [B() for _ in range(12)]
    b_q, b_sig, b_logf, b_kk, b_bb, b_eb, b_enb, b_qt, b_kt, b_sgl = [[B(), B()] for _ in range(10)]
    b_v, b_osq, b_tmp = B(), B(), B()
    b_kttok = [B(), B()]
    b_at = [B(), B()]
    b_T = [B(), B()]
    b_St = [B(), B()]
    b_Sbf = [[B(), B()], [B(), B()]]
    b_osb = [B(), B()]
    b_ogsb = [B(), B()]
    b_bank = [B() for _ in range(7)]
    b_bankT = B()

    P.op('pool', lambda e: e.memset(ones_bf[:], 1.0), writes=[b_const])
    P.op('pool', lambda e: e.memset(eps_sb[:], EPS), writes=[b_const])
    for i in range(2):
        P.op('pool', lambda e, i=i: e.memset(St[i][:], 0.0), writes=[b_St[i]])
        P.op('pool', lambda e, i=i: e.memset(Sbf[i][0][:], 0.0), writes=[b_Sbf[i][0]])
    P.dma('sp', g4_sb[:], g4[:, :], writes=[b_g4])
    P.dma('sp', gn_sb[:], gnorm[:, :], writes=[b_const])
    P.dma('sp', lbt_sb[:], lbt[:, :, :], writes=[b_lb])
    P.dma('sp', mask2_f[:], mask2_in[:, :], writes=[b_mask2])
    P.dma('sp', rmask[:], rmask_in[:, :], writes=[b_rmask])
    P.dma('sp', ident_f[:], ident_in[:, :], writes=[b_ident])
    P.op('dve', lambda e: e.tensor_copy(out=ident[:], in_=ident_f[:]), reads=[b_ident], writes=[b_ident])
    P.op('act', lambda e: e.activation(out=elb[:], in_=lbt_sb[:], func=AF.Exp), reads=[b_lb], writes=[b_lb])
    P.op('dve', lambda e: e.tensor_tensor(out=s01[:], in0=elb[:, 0, :], in1=elb[:, 1, :], op=ALU.add),
         reads=[b_lb], writes=[b_lb])
    P.op('dve', lambda e: e.tensor_tensor(out=sall[:], in0=s01[:], in1=elb[:, 2, :], op=ALU.add),
         reads=[b_lb], writes=[b_lb])
    P.op('dve', lambda e: e.reciprocal(out=rs[:], in_=sall[:]), reads=[b_lb], writes=[b_lb])
    P.op('dve', lambda e: e.tensor_tensor(out=lb[:], in0=s01[:], in1=rs[:], op=ALU.mult), reads=[b_lb], writes=[b_lb])
    P.op('dve', lambda e: e.tensor_tensor(out=oml[:], in0=elb[:, 2, :], in1=rs[:], op=ALU.mult),
         reads=[b_lb], writes=[b_lb])
    P.op('dve', lambda e: e.tensor_scalar(out=noml[:], in0=oml[:], scalar1=-1.0, scalar2=None, op0=ALU.mult),
         reads=[b_lb], writes=[b_lb])
    wv = w.rearrange("(kc p) n -> p kc n", p=128)
    for half in range(2):
        for kc in range(KC):
            P.dma('sp', wst[:, kc, :], wv[:, kc, half * 512:(half + 1) * 512], writes=[b_wst])
        for kc in range(KC):
            eng = 'dve' if kc % 2 == 0 else 'pool'
            P.op(eng, lambda e, kc=kc, half=half: e.tensor_scalar(
                out=Wb[:, kc, half * 512:(half + 1) * 512], in0=wst[:, kc, :], scalar1=g4_sb[:, kc:kc + 1],
                scalar2=None, op0=ALU.mult), reads=[b_wst, b_g4], writes=[b_Wb])

    xv = hT.rearrange("(kc p) t -> p kc t", p=128)
    pbk = [0]

    def nbank():
        k = pbk[0] % 3
        pbk[0] += 1
        return k
    atc = [0]
    kvc = [0]
    ver = [0, 0]

    def do_tile(j):
        tsl = slice(j * 512, (j + 1) * 512)
        for half in range(2):
            P.dma('sp', wst[:, half * 4:(half + 1) * 4, :], xv[:, half * 4:(half + 1) * 4, tsl], writes=[b_wst])
        P.op('act', lambda e: e.activation(out=sq[:], in_=wst[:], func=AF.Square), reads=[b_wst], writes=[b_sq])
        k = nbank()
        for kc in range(KC):
            P.op('pe', lambda e, kc=kc, k=k: e.matmul(bank[k][:, :], lhsT=ones_bf[:], rhs=sq[:, kc, :],
                                                      start=(kc == 0), stop=(kc == KC - 1)),
                 reads=[b_const, b_sq], writes=[b_bank[k]])
        P.op('act', lambda e, k=k: e.activation(out=lnr[:], in_=bank[k][:, :], func=AF.Ln, bias=eps_sb[:, 0:1],
                                                scale=1.0 / D), reads=[b_bank[k], b_const], writes=[b_lnr])
        P.op('act', lambda e: e.activation(out=rstd[:], in_=lnr[:], func=AF.Exp, scale=-0.5),
             reads=[b_lnr], writes=[b_rstd])
        for kc in range(KC):
            eng = 'dve' if kc % 2 == 0 else 'pool'
            P.op(eng, lambda e, kc=kc: e.tensor_tensor(out=hn[:, kc, :], in0=wst[:, kc, :], in1=rstd[:], op=ALU.mult),
                 reads=[b_wst, b_rstd], writes=[b_hn])

        def proj(col0):
            k = nbank()
            for kc in range(KC):
                P.op('pe', lambda e, kc=kc, k=k: e.matmul(bank[k][:, :], lhsT=Wb[:, kc, col0:col0 + 128],
                                                          rhs=hn[:, kc, :], start=(kc == 0), stop=(kc == KC - 1)),
                     reads=[b_Wb, b_hn], writes=[b_bank[k]])
            return k
        for hh in range(2):
            k = proj(hh * 128)
            P.op('act', lambda e, k=k, hh=hh: e.activation(out=q_sb[hh][:], in_=bank[k][:, :], func=AF.Copy),
                 reads=[b_bank[k]], writes=[b_q[hh]])
            k = proj(256 + hh * 128)
            P.op('act', lambda e, k=k, hh=hh: e.activation(out=sig[hh][:], in_=bank[k][:, :], func=AF.Sigmoid),
                 reads=[b_bank[k]], writes=[b_sig[hh]])
            P.op('act', lambda e, hh=hh: e.activation(out=logf[hh][:], in_=sig[hh][:], func=AF.Ln,
                                                      bias=lb[:, hh:hh + 1], scale=oml[:, hh:hh + 1]),
                 reads=[b_sig[hh], b_lb], writes=[b_logf[hh]])
            P.op('dve', lambda e, hh=hh: e.tensor_scalar(out=kk[hh][:], in0=sig[hh][:], scalar1=noml[:, hh:hh + 1],
                                                         scalar2=oml[:, hh:hh + 1], op0=ALU.mult, op1=ALU.add),
                 reads=[b_sig[hh], b_lb], writes=[b_kk[hh]])
            P.op('dve', lambda e, hh=hh: e.tensor_tensor_scan(out=bb[hh][:], data0=rmask[:], data1=logf[hh][:],
                                                              initial=0.0, op0=ALU.mult, op1=ALU.add),
                 reads=[b_rmask, b_logf[hh]], writes=[b_bb[hh]])
            P.op('act', lambda e, hh=hh: e.activation(out=eb[hh][:], in_=bb[hh][:], func=AF.Exp),
                 reads=[b_bb[hh]], writes=[b_eb[hh]])
            P.op('act', lambda e, hh=hh: e.activation(out=enb[hh][:], in_=bb[hh][:], func=AF.Exp, scale=-1.0),
                 reads=[b_bb[hh]], writes=[b_enb[hh]])
            P.op('dve', lambda e, hh=hh: e.tensor_tensor(out=qt[hh][:], in0=q_sb[hh][:], in1=eb[hh][:], op=ALU.mult),
                 reads=[b_q[hh], b_eb[hh]], writes=[b_qt[hh]])
            P.op('pool', lambda e, hh=hh: e.tensor_tensor(out=kt[hh][:], in0=kk[hh][:], in1=enb[hh][:], op=ALU.mult),
                 reads=[b_kk[hh], b_enb[hh]], writes=[b_kt[hh]])
            k = proj(768 + hh * 128)
            P.op('act', lambda e, k=k, hh=hh: e.activation(out=sgl[hh][:], in_=bank[k][:, :], func=AF.Silu),
                 reads=[b_bank[k]], writes=[b_sgl[hh]])
        for half in range(2):
            k = nbank()
            for s2 in range(2):
                sub = half * 2 + s2
                for kc in range(KC):
                    P.op('pe', lambda e, kc=kc, k=k, sub=sub, s2=s2: e.matmul(
                        bank[k][:, s2 * 256:(s2 + 1) * 256], lhsT=hn[:, kc, sub * 128:(sub + 1) * 128],
                        rhs=Wb[:, kc, 512:768], start=(kc == 0), stop=(kc == KC - 1)),
                        reads=[b_Wb, b_hn], writes=[b_bank[k]])
            P.op('dve', lambda e, k=k, half=half: e.tensor_copy(
                out=v_tok[:, half * 2:half * 2 + 2, :], in_=bank[k][:, :].rearrange("p (s n) -> p s n", s=2)),
                reads=[b_bank[k]], writes=[b_v])
        for hh in range(2):
            for sub in range(4):
                P.op('pe', lambda e, hh=hh, sub=sub: e.transpose(out=bankT[:, (hh * 4 + sub) * 128:(hh * 4 + sub + 1) * 128],
                                                                 in_=kt[hh][:, sub * 128:(sub + 1) * 128],
                                                                 identity=ident[:]),
                     reads=[b_kt[hh], b_ident], writes=[b_bankT])
            P.op('act', lambda e, hh=hh: e.activation(
                out=kt_tok[:, hh, :, :], in_=bankT[:, hh * 512:(hh + 1) * 512].rearrange("p (s n) -> p s n", s=4),
                func=AF.Copy), reads=[b_bankT], writes=[b_kttok[hh]])
        def do_sub(sub):
            csl = slice(sub * 128, (sub + 1) * 128)
            ai = {}
            for hh in range(2):
                a = atc[0] % 4
                atc[0] += 1
                ai[hh] = a
                P.op('pe', lambda e, hh=hh, a=a: e.matmul(bank[3][:, a * 128:(a + 1) * 128], lhsT=kt[hh][:, csl],
                                                          rhs=qt[hh][:, csl], start=True, stop=True),
                     reads=[b_kt[hh], b_qt[hh]], writes=[b_bank[3]])
            for hh in range(2):
                a = ai[hh]
                P.op('dve', lambda e, hh=hh, a=a: e.tensor_tensor(out=at_sb[hh][:], in0=bank[3][:, a * 128:(a + 1) * 128],
                                                                  in1=mask2_f[:], op=ALU.mult),
                     reads=[b_bank[3], b_mask2], writes=[b_at[hh]])
            for hh in range(2):
                P.op('pe', lambda e, hh=hh: e.matmul(bank[4 + hh][:, csl], lhsT=v_tok[:, sub, hh * 128:(hh + 1) * 128],
                                                     rhs=at_sb[hh][:], start=True, stop=False),
                     reads=[b_v, b_at[hh]], writes=[b_bank[4 + hh]])
            def do_c(c):
                ki = {}
                for hh in range(2):
                    vv = ver[hh]
                    c0 = sub * 128 + c * 64
                    P.op('pe', lambda e, hh=hh, vv=vv, c0=c0, c=c: e.matmul(
                        bank[4 + hh][:, c0:c0 + 64], lhsT=Sbf[hh][vv][:], rhs=qt[hh][:, c0:c0 + 64],
                        start=False, stop=(c == 1)), reads=[b_Sbf[hh][vv], b_qt[hh]], writes=[b_bank[4 + hh]])
                    ks = kvc[0] % 4
                    kvc[0] += 1
                    ki[hh] = ks
                    P.op('pe', lambda e, hh=hh, ks=ks, c=c: e.matmul(
                        bank[6][:, ks * 128:(ks + 1) * 128], lhsT=kt_tok[c * 64:(c + 1) * 64, hh, sub, :],
                        rhs=v_tok[c * 64:(c + 1) * 64, sub, hh * 128:(hh + 1) * 128], start=True, stop=True),
                        reads=[b_kttok[hh], b_v], writes=[b_bank[6]])
                for hh in range(2):
                    ks = ki[hh]
                    c0 = sub * 128 + c * 64
                    ecol = c0 + 63
                    P.op('act', lambda e, hh=hh, ks=ks, ecol=ecol: e.activation(
                        out=T_sb[hh][:], in_=bank[6][:, ks * 128:(ks + 1) * 128], func=AF.Copy,
                        scale=eb[hh][:, ecol:ecol + 1]), reads=[b_bank[6], b_eb[hh]], writes=[b_T[hh]])
                    P.op('dve', lambda e, hh=hh, ecol=ecol: e.scalar_tensor_tensor(
                        out=St[hh][:], in0=St[hh][:], scalar=eb[hh][:, ecol:ecol + 1], in1=T_sb[hh][:],
                        op0=ALU.mult, op1=ALU.add), reads=[b_St[hh], b_eb[hh], b_T[hh]], writes=[b_St[hh]])
                    nv = 1 - ver[hh]
                    P.op('pool', lambda e, hh=hh, nv=nv: e.tensor_copy(out=Sbf[hh][nv][:], in_=St[hh][:]),
                         reads=[b_St[hh]], writes=[b_Sbf[hh][nv]])
                    ver[hh] = nv
            do_c(0)
            do_c(1)
            for hh in range(2):
                P.op('act', lambda e, hh=hh: e.activation(out=o_sb[hh][:, csl], in_=bank[4 + hh][:, csl], func=AF.Copy),
                     reads=[b_bank[4 + hh]], writes=[b_osb[hh]])
        for sub in range(4):
            do_sub(sub)
        for hh in range(2):
            P.op('act', lambda e, hh=hh: e.activation(out=osq[:], in_=o_sb[hh][:], func=AF.Square),
                 reads=[b_osb[hh]], writes=[b_osq])
            k = nbank()
            P.op('pe', lambda e, k=k: e.matmul(bank[k][:, :], lhsT=ones_bf[:], rhs=osq[:], start=True, stop=True),
                 reads=[b_const, b_osq], writes=[b_bank[k]])
            P.op('act', lambda e, k=k: e.activation(out=lnr[:], in_=bank[k][:, :], func=AF.Ln, bias=eps_sb[:, 0:1],
                                                    scale=1.0 / 128), reads=[b_bank[k], b_const], writes=[b_lnr])
            P.op('act', lambda e: e.activation(out=rstd[:], in_=lnr[:], func=AF.Exp, scale=-0.5),
                 reads=[b_lnr], writes=[b_rstd])
            P.op('pool', lambda e, hh=hh: e.tensor_tensor(out=tmp[:], in0=o_sb[hh][:], in1=rstd[:], op=ALU.mult),
                 reads=[b_osb[hh], b_rstd], writes=[b_tmp])
            P.op('dve', lambda e, hh=hh: e.scalar_tensor_tensor(out=og_sb[hh][:], in0=tmp[:], scalar=gn_sb[:, 0:1],
                                                                in1=sgl[hh][:], op0=ALU.mult, op1=ALU.mult),
                 reads=[b_tmp, b_const, b_sgl[hh]], writes=[b_ogsb[hh]])
            P.dma('sp', og[hh * 128:(hh + 1) * 128, tsl], og_sb[hh][:], reads=[b_ogsb[hh]])

    for j in range(NT):
        do_tile(j)


SEQ = 8192
_CACHE = {}


def _build_pa():
    nc = bass.Bass("TRN2", target_bir_lowering=False)
    xT = nc.dram_tensor("xT", [1024, SEQ], F32, kind="ExternalInput").ap()
    w = nc.dram_tensor("w", [1024, 772], F32, kind="ExternalInput").ap()
    g0 = nc.dram_tensor("g0", [128, 8], F32, kind="ExternalInput").ap()
    bf = nc.dram_tensor("bf", [4, 1], F32, kind="ExternalInput").ap()
    tri_in = nc.dram_tensor("tri", [128, 128], F32, kind="ExternalInput").ap()
    out = nc.dram_tensor("attn", [256, SEQ], BF16, kind="ExternalOutput").ap()
    with ExitStack() as gs:
        P = Prog(nc, gs)
        with ExitStack() as st:
            build_phase_a(nc, P, st, SEQ, xT, w, g0, bf, tri_in, out)
            P.emit()
    return nc


def _build_pd():
    T = 2048
    nc = bass.Bass("TRN2", target_bir_lowering=False)
    aT = nc.dram_tensor("aT", [1024, T], BF16, kind="ExternalInput").ap()
    rT = nc.dram_tensor("rT", [1024, T], F32, kind="ExternalInput").ap()
    wo = nc.dram_tensor("wo", [1024, 1024], F32, kind="ExternalInput").ap()
    wfi = nc.dram_tensor("wfi", [1024, 5632], F32, kind="ExternalInput").ap()
    wfo = nc.dram_tensor("wfo", [2816, 1024], F32, kind="ExternalInput").ap()
    gains = nc.dram_tensor("gains", [128, 3, 8], F32, kind="ExternalInput").ap()
    out = nc.dram_tensor("outT", [1024, T], F32, kind="ExternalOutput").ap()
    with ExitStack() as gs:
        P = Prog(nc, gs)
        with ExitStack() as st:
            build_phase_d(nc, P, st, T, aT, rT, wo, wfi, wfo, gains, out)
            P.emit()
    return nc


def _build_pr():
    nc = bass.Bass("TRN2", target_bir_lowering=False)
    hT = nc.dram_tensor("hT", [1024, SEQ], F32, kind="ExternalInput").ap()
    w = nc.dram_tensor("w", [1024, 1024], F32, kind="ExternalInput").ap()
    g4d = nc.dram_tensor("g4", [128, 8], F32, kind="ExternalInput").ap()
    lbtd = nc.dram_tensor("lbt", [128, 3, 2], F32, kind="ExternalInput").ap()
    gnd = nc.dram_tensor("gn", [128, 1], F32, kind="ExternalInput").ap()
    m2 = nc.dram_tensor("mask2", [128, 128], F32, kind="ExternalInput").ap()
    rm = nc.dram_tensor("rmask", [128, 512], F32, kind="ExternalInput").ap()
    idd = nc.dram_tensor("ident", [128, 128], F32, kind="ExternalInput").ap()
    og = nc.dram_tensor("og", [256, SEQ], BF16, kind="ExternalOutput").ap()
    with ExitStack() as gs:
        P = Prog(nc, gs)
        with ExitStack() as st:
            build_phase_r(nc, P, st, SEQ, hT, w, g4d, lbtd, gnd, m2, rm, idd, og)
            P.emit()
    return nc


def _get(name, fn):
    if name not in _CACHE:
        _CACHE[name] = fn()
    return _CACHE[name]


def _pk(g):
    return np.ascontiguousarray(np.asarray(g, np.float32).reshape(8, 128).T)


def kernel(x, fox_w_in, fox_b_f, fox_w_out, hgrn_w_in, hgrn_lb_table, hgrn_gnorm, hgrn_w_out, ffn_w_in,
           ffn_w_out, norm_gains):
    x = np.asarray(x, np.float32)
    fox_w_in = np.asarray(fox_w_in, np.float32)
    hgrn_w_in = np.asarray(hgrn_w_in, np.float32)
    norm_gains = np.asarray(norm_gains, np.float32)
    ffn_w_in = np.asarray(ffn_w_in, np.float32)
    ffn_w_out = np.asarray(ffn_w_out, np.float32)
    cores = list(range(8))
    xT = [np.ascontiguousarray(x[b].T) for b in range(2)]
    tri = (np.arange(128)[None, :] >= np.arange(128)[:, None]).astype(np.float32)
    ii = np.arange(128)
    mask2 = ((ii[:, None] // 64 == ii[None, :] // 64) & (ii[None, :] >= ii[:, None])).astype(np.float32)
    rmask = np.ascontiguousarray((np.arange(512) % 64 != 0).astype(np.float32)[None, :].repeat(128, 0))
    ident = np.eye(128, dtype=np.float32)

    in_maps = []
    for c in cores:
        b, g = c // 4, c % 4
        heads = [4 * g + i for i in range(4)]
        cols = lambda base: np.concatenate([np.arange(base + h * 64, base + (h + 1) * 64) for h in heads])
        wsel = np.concatenate([fox_w_in[0][:, cols(0)], fox_w_in[0][:, cols(1024)], fox_w_in[0][:, cols(2048)],
                               fox_w_in[0][:, 3072 + np.array(heads)]], axis=1)
        in_maps.append({"xT": xT[b], "w": np.ascontiguousarray(wsel), "g0": _pk(norm_gains[0, 0]),
                        "bf": np.ascontiguousarray(np.asarray(fox_b_f, np.float32)[0, heads].reshape(4, 1)),
                        "tri": tri})
    res = run_bass_kernel_spmd(_get('pa', _build_pa), in_maps, core_ids=cores)
    attnT = [np.concatenate([np.asarray(res.results[b * 4 + g]["attn"]) for g in range(4)], axis=0)
             for b in range(2)]

    def dense(aT, rT, w_o, w_fi, w_fo, g3):
        gains = np.ascontiguousarray(np.asarray(g3, np.float32).reshape(3, 8, 128).transpose(2, 0, 1))
        in_maps = []
        for c in cores:
            b, r = c // 4, c % 4
            sl = slice(r * 2048, (r + 1) * 2048)
            in_maps.append({"aT": np.ascontiguousarray(aT[b][:, sl]), "rT": np.ascontiguousarray(rT[b][:, sl]),
                            "wo": np.ascontiguousarray(w_o), "wfi": np.ascontiguousarray(w_fi),
                            "wfo": np.ascontiguousarray(w_fo), "gains": gains})
        res = run_bass_kernel_spmd(_get('pd', _build_pd), in_maps, core_ids=cores)
        return [np.concatenate([np.asarray(res.results[b * 4 + r]["outT"]) for r in range(4)], axis=1)
                for b in range(2)]

    h1T = dense(attnT, xT, np.asarray(fox_w_out, np.float32)[0], ffn_w_in[0], ffn_w_out[0], norm_gains[0, 1:4])

    lbtab = np.asarray(hgrn_lb_table, np.float32)
    in_maps = []
    for c in cores:
        b, hp = c // 4, c % 4
        heads = [2 * hp, 2 * hp + 1]
        cols = lambda base: np.concatenate([np.arange(base + h * 128, base + (h + 1) * 128) for h in heads])
        wsel = np.concatenate([hgrn_w_in[0][:, cols(0)], hgrn_w_in[0][:, cols(1024)], hgrn_w_in[0][:, cols(2048)],
                               hgrn_w_in[0][:, cols(3072)]], axis=1)
        lbt = np.stack([lbtab[:, h * 128:(h + 1) * 128] for h in heads], axis=-1)
        in_maps.append({"hT": h1T[b], "w": np.ascontiguousarray(wsel), "g4": _pk(norm_gains[1, 0]),
                        "lbt": np.ascontiguousarray(lbt.transpose(1, 0, 2)),
                        "gn": np.ascontiguousarray(np.asarray(hgrn_gnorm, np.float32)[0].reshape(128, 1)),
                        "mask2": mask2, "rmask": rmask, "ident": ident})
    res = run_bass_kernel_spmd(_get('pr', _build_pr), in_maps, core_ids=cores)
    ogT = [np.concatenate([np.asarray(res.results[b * 4 + hp]["og"]) for hp in range(4)], axis=0)
           for b in range(2)]

    outT = dense(ogT, h1T, np.asarray(hgrn_w_out, np.float32)[0], ffn_w_in[1], ffn_w_out[1], norm_gains[1, 1:4])
    return np.ascontiguousarray(np.stack([outT[b].T for b in range(2)], axis=0)).astype(np.float32)
```

```python
import numpy as np
import ml_dtypes
from contextlib import ExitStack
from concourse.bass_utils import run_bass_kernel_spmd
import concourse.bass as bass
import concourse.mybir as mybir

F32 = mybir.dt.float32
BF16 = mybir.dt.bfloat16
AF = mybir.ActivationFunctionType
ALU = mybir.AluOpType

BLOCKNAME = {'pe': 'tensor', 'act': 'scalar', 'dve': 'vector', 'pool': 'gpsimd', 'sp': 'sync'}
ENGS = ['pe', 'act', 'dve', 'pool', 'sp']
NDS = 8


class Buf:
    __slots__ = ('name', 'w', 'rs')

    def __init__(self, name=''):
        self.name = name
        self.w = None
        self.rs = []


class Op:
    __slots__ = ('eng', 'fn', 'deps', 'marked', 'semval', 'dma', 'dsem', 'dval', 'emitted')

    def __init__(self, eng, fn, dma=False):
        self.eng = eng
        self.fn = fn
        self.deps = []
        self.marked = False
        self.semval = None
        self.dma = dma
        self.dsem = None
        self.dval = None
        self.emitted = False


class Prog:
    def __init__(self, nc, stack):
        self.nc = nc
        self.sem = {e: stack.enter_context(nc.semaphore('sem_' + e)) for e in ENGS}
        self.dsems = {e: [stack.enter_context(nc.semaphore('dsem_%s_%d' % (e, k))) for k in range(NDS)]
                      for e in ENGS}
        self.dcount = {e: [0] * NDS for e in ENGS}
        self.drr = {e: 0 for e in ENGS}
        self.count = {e: 0 for e in ENGS}
        self.waited = {e: {} for e in ENGS}
        self.ops = {e: [] for e in ENGS}
        self.nops = 0

    def _add(self, op, reads, writes):
        deps = {}
        for b in reads:
            if b.w is not None:
                deps[id(b.w)] = (b.w, 'raw')
        for b in writes:
            if b.w is not None:
                deps[id(b.w)] = (b.w, 'waw')
            for r in b.rs:
                if id(r) not in deps:
                    deps[id(r)] = (r, 'war')
        for p, kind in deps.values():
            if p is op:
                continue
            if not p.dma and not op.dma and p.eng == op.eng:
                if op.eng == 'pe':
                    continue
                if kind == 'war':
                    continue
            op.deps.append(p)
            if not p.dma:
                p.marked = True
        for b in reads:
            b.rs.append(op)
        for b in writes:
            b.w = op
            b.rs = []
        self.ops[op.eng].append(op)
        self.nops += 1
        return op

    def op(self, eng, fn, reads=(), writes=()):
        return self._add(Op(eng, fn), list(reads), list(writes))

    def dma(self, q, out, in_, reads=(), writes=(), **kw):
        def fn(e):
            return e.dma_start(out=out, in_=in_, **kw)
        return self._add(Op(q, fn, dma=True), list(reads), list(writes))

    def _wait(self, engobj, e, sem, key, val):
        w = self.waited[e]
        if w.get(key, 0) >= val:
            return
        engobj.wait_ge(sem, val)
        w[key] = val

    def _emit_op(self, engobj, e, op):
        for p in op.deps:
            if p.dma:
                assert p.emitted or p.dval is not None, 'dma dep not assigned'
                self._wait(engobj, e, p.dsem, ('d', id(p.dsem)), p.dval)
            else:
                assert p.semval is not None, 'dep on unassigned op'
                self._wait(engobj, e, self.sem[p.eng], ('e', p.eng), p.semval)
        if op.dma:
            if op.dval > 16:
                self._wait(engobj, e, op.dsem, ('d', id(op.dsem)), op.dval - 16)
            ins = op.fn(engobj)
            ins.then_inc(op.dsem, 16)
        else:
            ins = op.fn(engobj)
            if op.marked:
                ins.then_inc(self.sem[e], 1)
        op.emitted = True

    def emit(self):
        for e in ENGS:
            for op in self.ops[e]:
                if op.dma:
                    k = self.drr[e]
                    self.drr[e] = (k + 1) % NDS
                    self.dcount[e][k] += 16
                    op.dsem = self.dsems[e][k]
                    op.dval = self.dcount[e][k]
                elif op.marked:
                    self.count[e] += 1
                    op.semval = self.count[e]
        with self.nc.Block() as block:
            for e in ENGS:
                ops = self.ops[e]
                if not ops:
                    continue

                def body(engobj, e=e, ops=ops):
                    for op in ops:
                        self._emit_op(engobj, e, op)
                    for k in range(NDS):
                        if self.dcount[e][k] > 0:
                            self._wait(engobj, e, self.dsems[e][k], ('d', id(self.dsems[e][k])),
                                       self.dcount[e][k])
                getattr(block, BLOCKNAME[e])(body)
        self.ops = {e: [] for e in ENGS}


NH = 4
HD = 64
D = 1024
KC = 8
EPS = 1e-6


def build_phase_a(nc, P, stack, S, xT, w, g0, bf, tri_in, attn_out):
    NT = S // 512
    NKB = S // 128
    sb = lambda name, shape, dt: stack.enter_context(nc.sbuf_tensor(name, shape, dt))
    ps = lambda name, shape, dt: stack.enter_context(nc.psum_tensor(name, shape, dt))

    ones_bf = sb('ones_bf', [128, 128], BF16)
    tri_f = sb('tri_f', [128, 128], F32)
    tri_bf = sb('tri_bf', [128, 128], BF16)
    sel = sb('sel', [65, 64], F32)
    g0_sb = sb('g0_sb', [128, 8], F32)
    bf_sb = sb('bf_sb', [4, 1], F32)
    negb = sb('negb', [4, 1], F32)
    eps_sb = sb('eps_sb', [128, 1], F32)
    one_sb = sb('one_sb', [128, 1], F32)
    ones4 = sb('ones4', [4, 512], F32)
    Wb = sb('Wb', [128, KC, 772], BF16)
    xf = [sb('xf%d' % i, [128, KC, 512], F32) for i in range(2)]
    sq = sb('sq', [128, KC, 512], BF16)
    xb = [sb('xb%d' % i, [128, KC, 512], BF16) for i in range(2)]
    lnr = sb('lnr', [128, 512], F32)
    rstd = sb('rstd', [128, 512], F32)
    qaug = [sb('qaug%d' % i, [70, NH, 512], BF16) for i in range(2)]
    kaug = sb('kaug', [70, NH, S], BF16)
    vaug = sb('vaug', [128, NKB, NH, 65], BF16)
    ef = sb('ef', [4, 512], F32)
    lf = sb('lf', [4, 512], F32)
    Lc = [sb('Lc%d' % i, [4, 512], F32) for i in range(2)]
    r1 = sb('r1', [4, 512], F32)
    r2 = sb('r2', [4, 512], F32)
    pc = [sb('pc%d' % i, [4, 3, 512], BF16) for i in range(2)]
    pT = [sb('pT%d' % i, [128, 512], BF16) for i in range(3)]
    o_sb = [sb('o_sb%d' % i, [65, 512], F32) for i in range(2)]
    a_sb = [sb('a_sb%d' % i, [64, 512], BF16) for i in range(2)]
    bank = [ps('bank%d' % i, [128, 512], F32) for i in range(8)]

    B = Buf
    b_ones, b_tri, b_sel, b_g0, b_bf, b_negb, b_eps, b_wst, b_Wb = [B() for _ in range(9)]
    b_xf = [B(), B()]
    b_sq = B()
    b_xb = [B(), B()]
    b_lnr, b_rstd = B(), B()
    b_qaug = [[B() for h in range(NH)] for i in range(2)]
    b_qaugL = [[B() for h in range(NH)] for i in range(2)]
    b_k = [[B() for h in range(NH)] for j in range(NT)]
    b_kL = [[B() for h in range(NH)] for j in range(NT)]
    b_v = [B() for j in range(NT)]
    b_ef, b_lf, b_r1, b_r2 = B(), B(), B(), B()
    b_Lc = [B() for j in range(NT)]
    b_pc = [B() for j in range(NT)]
    b_pT = [B() for i in range(3)]
    b_osb = [B(), B()]
    b_asb = [B(), B()]
    b_bank = [B() for i in range(8)]
    b_misc = B()

    P.op('pool', lambda e: e.memset(ones_bf[:], 1.0), writes=[b_ones])
    P.op('pool', lambda e: e.memset(eps_sb[:], EPS), writes=[b_eps])
    P.op('pool', lambda e: e.memset(one_sb[:], 1.0), writes=[b_eps])
    P.op('pool', lambda e: e.memset(ones4[:], 1.0), writes=[b_misc])
    P.op('pool', lambda e: e.memset(sel[:], 0.0), writes=[b_sel])
    P.op('pool', lambda e: e.memset(sel[64:65, :], 1.0), writes=[b_sel])
    P.op('pool', lambda e: e.memset(vaug[:], 1.0), writes=b_v)
    P.op('pool', lambda e: e.memset(kaug[64:70, :, :], -1.0), writes=[x for j in range(NT) for x in b_kL[j]])
    for i in range(2):
        P.op('pool', lambda e, i=i: e.memset(qaug[i][64:70, :, :], 1.0), writes=b_qaugL[i])
    P.dma('sp', tri_f[:], tri_in[:, :], writes=[b_tri])
    P.op('dve', lambda e: e.tensor_copy(out=tri_bf[:], in_=tri_f[:]), reads=[b_tri], writes=[b_tri])
    P.dma('sp', g0_sb[:], g0[:, :], writes=[b_g0])
    P.dma('sp', bf_sb[:], bf[:, :], writes=[b_bf])
    P.op('dve', lambda e: e.tensor_scalar(out=negb[:], in0=bf_sb[:], scalar1=-1.0, scalar2=None, op0=ALU.mult),
         reads=[b_bf], writes=[b_negb])
    wv = w.rearrange("(kc p) n -> p kc n", p=128)
    for kc in range(KC):
        P.dma('sp', xf[0][:, kc, :], wv[:, kc, 0:512], writes=[b_xf[0]])
        P.dma('sp', xf[1][:, kc, 0:260], wv[:, kc, 512:772], writes=[b_xf[1]])
    for kc in range(KC):
        P.op('dve', lambda e, kc=kc: e.tensor_scalar(out=Wb[:, kc, 0:256], in0=xf[0][:, kc, 0:256],
                                                     scalar1=g0_sb[:, kc:kc + 1], scalar2=0.125,
                                                     op0=ALU.mult, op1=ALU.mult),
             reads=[b_xf[0], b_g0], writes=[b_Wb])
        P.op('dve', lambda e, kc=kc: e.tensor_scalar(out=Wb[:, kc, 256:512], in0=xf[0][:, kc, 256:512],
                                                      scalar1=g0_sb[:, kc:kc + 1], scalar2=None,
                                                      op0=ALU.mult),
             reads=[b_xf[0], b_g0], writes=[b_Wb])
        P.op('dve', lambda e, kc=kc: e.tensor_scalar(out=Wb[:, kc, 512:772], in0=xf[1][:, kc, 0:260],
                                                      scalar1=g0_sb[:, kc:kc + 1], scalar2=None,
                                                      op0=ALU.mult),
             reads=[b_xf[1], b_g0], writes=[b_Wb])

    xv = xT.rearrange("(kc p) t -> p kc t", p=128)
    pb = [0]

    def next_pbank():
        k = pb[0]
        pb[0] = 1 - k
        return k

    def proj_stage(j):
        bi = j % 2
        tsl = slice(j * 512, (j + 1) * 512)
        for half in range(2):
            P.dma('sp', xf[bi][:, half * 4:(half + 1) * 4, :], xv[:, half * 4:(half + 1) * 4, tsl],
                  writes=[b_xf[bi]])
        P.op('act', lambda e: e.activation(out=sq[:], in_=xf[bi][:], func=AF.Square),
             reads=[b_xf[bi]], writes=[b_sq])
        k = next_pbank()
        for kc in range(KC):
            P.op('pe', lambda e, kc=kc, k=k: e.matmul(bank[k][:, :], lhsT=ones_bf[:], rhs=sq[:, kc, :],
                                                      start=(kc == 0), stop=(kc == KC - 1)),
                 reads=[b_ones, b_sq], writes=[b_bank[k]])
        P.op('act', lambda e, k=k: e.activation(out=lnr[:], in_=bank[k][:, :], func=AF.Ln,
                                                bias=eps_sb[:, 0:1], scale=1.0 / D),
             reads=[b_bank[k], b_eps], writes=[b_lnr])
        P.op('act', lambda e: e.activation(out=rstd[:], in_=lnr[:], func=AF.Exp, scale=-0.5),
             reads=[b_lnr], writes=[b_rstd])
        for kc in range(KC):
            eng = 'dve'
            P.op(eng, lambda e, kc=kc: e.tensor_tensor(out=xb[bi][:, kc, :], in0=xf[bi][:, kc, :],
                                                       in1=rstd[:], op=ALU.mult),
                 reads=[b_xf[bi], b_rstd], writes=[b_xb[bi]])
        for h in range(NH):
            k = next_pbank()
            for kc in range(KC):
                P.op('pe', lambda e, kc=kc, k=k, h=h: e.matmul(
                    bank[k][0:64, :], lhsT=Wb[:, kc, h * 64:(h + 1) * 64], rhs=xb[bi][:, kc, :],
                    start=(kc == 0), stop=(kc == KC - 1)),
                    reads=[b_Wb, b_xb[bi]], writes=[b_bank[k]])
            P.op('act', lambda e, k=k, h=h: e.activation(out=qaug[bi][0:64, h, :], in_=bank[k][0:64, :],
                                                        func=AF.Copy),
                 reads=[b_bank[k]], writes=[b_qaug[bi][h]])
            k = next_pbank()
            for kc in range(KC):
                P.op('pe', lambda e, kc=kc, k=k, h=h: e.matmul(
                    bank[k][0:64, :], lhsT=Wb[:, kc, 256 + h * 64:256 + (h + 1) * 64], rhs=xb[bi][:, kc, :],
                    start=(kc == 0), stop=(kc == KC - 1)),
                    reads=[b_Wb, b_xb[bi]], writes=[b_bank[k]])
            P.op('dve', lambda e, k=k, h=h: e.tensor_copy(out=kaug[0:64, h, tsl], in_=bank[k][0:64, :]),
                 reads=[b_bank[k]], writes=[b_k[j][h]])
        for half in range(2):
            k = next_pbank()
            for s2 in range(2):
                sub = half * 2 + s2
                for kc in range(KC):
                    P.op('pe', lambda e, kc=kc, k=k, sub=sub, s2=s2: e.matmul(
                        bank[k][:, s2 * 256:(s2 + 1) * 256], lhsT=xb[bi][:, kc, sub * 128:(sub + 1) * 128],
                        rhs=Wb[:, kc, 512:768], start=(kc == 0), stop=(kc == KC - 1)),
                        reads=[b_Wb, b_xb[bi]], writes=[b_bank[k]])
            for s2 in range(2):
                sub = half * 2 + s2
                eng = 'act' if s2 == 0 else 'dve'
                if eng == 'act':
                    P.op('act', lambda e, k=k, sub=sub, s2=s2: e.activation(
                        out=vaug[:, 4 * j + sub, :, 0:64],
                        in_=bank[k][:, s2 * 256:(s2 + 1) * 256].rearrange("p (h d) -> p h d", h=NH),
                        func=AF.Copy), reads=[b_bank[k]], writes=[b_v[j]])
                else:
                    P.op('dve', lambda e, k=k, sub=sub, s2=s2: e.tensor_copy(
                        out=vaug[:, 4 * j + sub, :, 0:64],
                        in_=bank[k][:, s2 * 256:(s2 + 1) * 256].rearrange("p (h d) -> p h d", h=NH)),
                        reads=[b_bank[k]], writes=[b_v[j]])
        k = next_pbank()
        for kc in range(KC):
            P.op('pe', lambda e, kc=kc, k=k: e.matmul(bank[k][0:4, :], lhsT=Wb[:, kc, 768:772],
                                                      rhs=xb[bi][:, kc, :], start=(kc == 0),
                                                      stop=(kc == KC - 1)),
                 reads=[b_Wb, b_xb[bi]], writes=[b_bank[k]])
        P.op('act', lambda e, k=k: e.activation(out=ef[:], in_=bank[k][0:4, :], func=AF.Exp,
                                                bias=negb[:, 0:1], scale=-1.0),
             reads=[b_bank[k], b_negb], writes=[b_ef])
        P.op('act', lambda e: e.activation(out=lf[:], in_=ef[:], func=AF.Ln, bias=one_sb[0:4, 0:1], scale=1.0),
             reads=[b_ef, b_eps], writes=[b_lf])
        init = 0.0 if j == 0 else Lc[1 - bi][:, 511:512]
        rd = [b_lf, b_misc] + ([b_Lc[1 - bi]] if j > 0 else [])
        P.op('dve', lambda e: e.tensor_tensor_scan(out=Lc[bi][:, :], data0=ones4[:], data1=lf[:], initial=init,
                                                   op0=ALU.mult, op1=ALU.add),
             reads=rd, writes=[b_Lc[bi]])
        P.op('dve', lambda e: e.tensor_copy(out=pc[bi][:, 0, :], in_=Lc[bi][:, :]), reads=[b_Lc[bi]], writes=[b_pc[bi]])
        P.op('dve', lambda e: e.tensor_tensor(out=r1[:], in0=Lc[bi][:, :], in1=pc[bi][:, 0, :], op=ALU.subtract),
             reads=[b_Lc[bi], b_pc[bi]], writes=[b_r1])
        P.op('dve', lambda e: e.tensor_copy(out=pc[bi][:, 1, :], in_=r1[:]), reads=[b_r1], writes=[b_pc[bi]])
        P.op('dve', lambda e: e.tensor_tensor(out=r2[:], in0=r1[:], in1=pc[bi][:, 1, :], op=ALU.subtract),
             reads=[b_r1, b_pc[bi]], writes=[b_r2])
        P.op('dve', lambda e: e.tensor_copy(out=pc[bi][:, 2, :], in_=r2[:]), reads=[b_r2], writes=[b_pc[bi]])
        for h in range(NH):
            for i in range(3):
                P.dma('sp', qaug[bi][64 + i:65 + i, h, :], pc[bi][h:h + 1, i, :],
                      reads=[b_pc[bi]], writes=[b_qaugL[bi][h]])
                P.dma('sp', kaug[67 + i:68 + i, h, tsl], pc[bi][h:h + 1, i, :],
                      reads=[b_pc[bi]], writes=[b_kL[j][h]])

    stc = [0]

    def attn_stage(j):
        bi = j % 2
        tsl = slice(j * 512, (j + 1) * 512)
        blocks = []
        for h in range(NH):
            nb = 4 * j + 4
            for kb in range(nb):
                i = kb - 4 * j
                col0 = 128 * i if i > 0 else 0
                blocks.append((h, kb, col0, i >= 0, kb == 0, kb == nb - 1))
        nblk = len(blocks)
        sbank = {}

        def emit_S(n):
            h, kb, col0, diag, first, last = blocks[n]
            s = stc[0] % 3
            stc[0] += 1
            sbank[n] = s
            jj = kb // 4
            P.op('pe', lambda e: e.matmul(bank[3 + s][:, col0:512], lhsT=kaug[0:70, h, kb * 128:(kb + 1) * 128],
                                          rhs=qaug[bi][0:70, h, col0:512], start=True, stop=True),
                 reads=[b_k[jj][h], b_kL[jj][h], b_qaug[bi][h], b_qaugL[bi][h]], writes=[b_bank[3 + s]])
            P.op('act', lambda e: e.activation(out=pT[s][:, col0:512], in_=bank[3 + s][:, col0:512], func=AF.Exp),
                 reads=[b_bank[3 + s]], writes=[b_pT[s]])
            if diag:
                P.op('dve', lambda e: e.tensor_tensor(out=pT[s][:, col0:col0 + 128], in0=pT[s][:, col0:col0 + 128],
                                                       in1=tri_bf[:], op=ALU.mult),
                     reads=[b_pT[s], b_tri], writes=[b_pT[s]])

        def emit_PV(n):
            h, kb, col0, diag, first, last = blocks[n]
            s = sbank[n]
            ob = 6 + (h % 2)
            jj = kb // 4
            P.op('pe', lambda e: e.matmul(bank[ob][0:65, col0:512], lhsT=vaug[:, kb, h, :], rhs=pT[s][:, col0:512],
                                          start=first, stop=last),
                 reads=[b_v[jj], b_pT[s]], writes=[b_bank[ob]])
            if last:
                oi = h % 2
                P.op('act', lambda e: e.activation(out=o_sb[oi][:], in_=bank[ob][0:65, :], func=AF.Copy),
                     reads=[b_bank[ob]], writes=[b_osb[oi]])
                P.op('dve', lambda e: e.reciprocal(out=o_sb[oi][64:65, :], in_=o_sb[oi][64:65, :]),
                     reads=[b_osb[oi]], writes=[b_osb[oi]])
                P.op('pe', lambda e: e.matmul(bank[2][0:64, :], lhsT=sel[:], rhs=o_sb[oi][:], start=True, stop=True),
                     reads=[b_sel, b_osb[oi]], writes=[b_bank[2]])
                P.op('dve', lambda e: e.tensor_tensor(out=a_sb[oi][:], in0=o_sb[oi][0:64, :], in1=bank[2][0:64, :],
                                                      op=ALU.mult),
                     reads=[b_osb[oi], b_bank[2]], writes=[b_asb[oi]])
                P.dma('sp', attn_out[h * 64:(h + 1) * 64, tsl], a_sb[oi][:], reads=[b_asb[oi]])

        emit_S(0)
        if nblk > 1:
            emit_S(1)
        for n in range(nblk):
            if n + 2 < nblk:
                emit_S(n + 2)
            emit_PV(n)

    for j in range(NT):
        proj_stage(j)
        attn_stage(j)


DFF = 2816
FC = 22


def build_phase_d(nc, P, stack, T, aT, resT, w_o, w_fi, w_fo, gains, outT):
    NTT = T // 512
    sb = lambda name, shape, dt: stack.enter_context(nc.sbuf_tensor(name, shape, dt))
    ps = lambda name, shape, dt: stack.enter_context(nc.psum_tensor(name, shape, dt))
    B = Buf
    ones_bf = sb('ones_bf', [128, 128], BF16)
    eps_sb = sb('eps_sb', [128, 1], F32)
    g_sb = sb('g_sb', [128, 3, 8], F32)
    a_sb = sb('a_sb', [128, KC, 512], BF16)
    r_sb = sb('r_sb', [128, KC, 512], F32)
    mixed = sb('mixed', [128, KC, 512], F32)
    h_sb = sb('h_sb', [128, KC, 512], F32)
    hn = sb('hn', [128, KC, 512], BF16)
    sq = sb('sq', [128, KC, 512], BF16)
    lnr = sb('lnr', [128, 512], F32)
    rstd = sb('rstd', [128, 512], F32)
    tmp = [sb('tmp%d' % i, [128, 512], F32) for i in range(2)]
    sg = [sb('sg%d' % i, [128, 512], F32) for i in range(2)]
    act = sb('act', [128, FC, 512], BF16)
    wst = [sb('wst%d' % i, [128, FC, 256], F32) for i in range(2)]
    wb = [sb('wb%d' % i, [128, FC, 256], BF16) for i in range(3)]
    bank = [ps('bank%d' % i, [128, 512], F32) for i in range(8)]
    b_ones, b_eps, b_g, b_a, b_r, b_mixed, b_h, b_hn, b_sq, b_lnr, b_rstd, b_act = [B() for _ in range(12)]
    b_tmp = [B(), B()]
    b_sg = [B(), B()]
    b_wst = [B(), B()]
    b_wb = [B(), B(), B()]
    b_bank = [B() for _ in range(8)]

    P.op('pool', lambda e: e.memset(ones_bf[:], 1.0), writes=[b_ones])
    P.op('pool', lambda e: e.memset(eps_sb[:], EPS), writes=[b_eps])
    P.dma('sp', g_sb[:], gains[:, :, :], writes=[b_g])

    wcnt = [0]
    wbc = [0]

    def stream_w(wd, nk, col0, ncols, gi):
        si = wcnt[0] % 2
        wcnt[0] += 1
        bi = wbc[0] % 3
        wbc[0] += 1
        wv = wd.rearrange("(kc p) n -> p kc n", p=128)
        step = 8
        for k0 in range(0, nk, step):
            k1 = min(nk, k0 + step)
            P.dma('sp', wst[si][:, k0:k1, 0:ncols], wv[:, k0:k1, col0:col0 + ncols], writes=[b_wst[si]])
        if gi is None:
            half = nk // 2
            P.op('dve', lambda e: e.tensor_copy(out=wb[bi][:, 0:half, 0:ncols], in_=wst[si][:, 0:half, 0:ncols]),
                 reads=[b_wst[si]], writes=[b_wb[bi]])
            P.op('dve', lambda e: e.tensor_copy(out=wb[bi][:, half:nk, 0:ncols], in_=wst[si][:, half:nk, 0:ncols]),
                 reads=[b_wst[si]], writes=[b_wb[bi]])
        else:
            for kc in range(nk):
                if kc % 3 == 2:
                    P.op('act', lambda e, kc=kc: e.activation(out=wb[bi][:, kc, 0:ncols], in_=wst[si][:, kc, 0:ncols],
                                                              func=AF.Copy, scale=g_sb[:, gi, kc:kc + 1]),
                         reads=[b_wst[si], b_g], writes=[b_wb[bi]])
                else:
                    P.op('dve', lambda e, kc=kc: e.tensor_scalar(out=wb[bi][:, kc, 0:ncols], in0=wst[si][:, kc, 0:ncols],
                                                                 scalar1=g_sb[:, gi, kc:kc + 1], scalar2=None,
                                                                 op0=ALU.mult),
                         reads=[b_wst[si], b_g], writes=[b_wb[bi]])
        return wb[bi], b_wb[bi]

    pbk = [0]

    def nbank():
        k = pbk[0] % 6
        pbk[0] += 1
        return k

    def rms_stats(src_buf):
        for kc in range(KC):
            P.op('pe', lambda e, kc=kc: e.matmul(bank[7][:, :], lhsT=ones_bf[:], rhs=sq[:, kc, :],
                                                 start=(kc == 0), stop=(kc == KC - 1)),
                 reads=[b_ones, b_sq], writes=[b_bank[7]])
        P.op('act', lambda e: e.activation(out=lnr[:], in_=bank[7][:, :], func=AF.Ln, bias=eps_sb[:, 0:1],
                                           scale=1.0 / D), reads=[b_bank[7], b_eps], writes=[b_lnr])
        P.op('act', lambda e: e.activation(out=rstd[:], in_=lnr[:], func=AF.Exp, scale=-0.5),
             reads=[b_lnr], writes=[b_rstd])

    def post_norm_residual(y, b_y, gi, res, b_res, out, b_out):
        for mc in range(KC):
            ti = mc % 2
            P.op('dve', lambda e, mc=mc, ti=ti: e.tensor_tensor(out=tmp[ti][:], in0=y[:, mc, :], in1=rstd[:],
                                                                op=ALU.mult),
                 reads=[b_y, b_rstd], writes=[b_tmp[ti]])
            P.op('dve', lambda e, mc=mc, ti=ti: e.tensor_tensor(out=out[:, mc, :], in0=tmp[ti][:],
                                                                in1=res[:, mc, :], op=ALU.add),
                 reads=[b_tmp[ti], b_res], writes=[b_out])

    aV = aT.rearrange("(kc p) t -> p kc t", p=128)
    rV = resT.rearrange("(kc p) t -> p kc t", p=128)
    oV = outT.rearrange("(kc p) t -> p kc t", p=128)

    for t in range(NTT):
        tsl = slice(t * 512, (t + 1) * 512)
        for half in range(2):
            P.dma('sp', a_sb[:, half * 4:(half + 1) * 4, :], aV[:, half * 4:(half + 1) * 4, tsl], writes=[b_a])
            P.dma('sp', r_sb[:, half * 4:(half + 1) * 4, :], rV[:, half * 4:(half + 1) * 4, tsl], writes=[b_r])
        for cb in range(4):
            wt, bw = stream_w(w_o, KC, cb * 256, 256, None)
            for m2 in range(2):
                mc = cb * 2 + m2
                k = nbank()
                for kc in range(KC):
                    P.op('pe', lambda e, kc=kc, k=k, m2=m2, wt=wt: e.matmul(
                        bank[k][:, :], lhsT=wt[:, kc, m2 * 128:(m2 + 1) * 128], rhs=a_sb[:, kc, :],
                        start=(kc == 0), stop=(kc == KC - 1)), reads=[bw, b_a], writes=[b_bank[k]])
                P.op('act', lambda e, k=k, mc=mc: e.activation(out=mixed[:, mc, :], in_=bank[k][:, :], func=AF.Copy,
                                                               scale=g_sb[:, 0, mc:mc + 1]),
                     reads=[b_bank[k], b_g], writes=[b_mixed])
                P.op('act', lambda e, k=k, mc=mc: e.activation(out=sq[:, mc, :], in_=bank[k][:, :], func=AF.Square),
                     reads=[b_bank[k]], writes=[b_sq])
        rms_stats(None)
        post_norm_residual(mixed, b_mixed, 0, r_sb, b_r, h_sb, b_h)
        P.op('act', lambda e: e.activation(out=sq[:], in_=h_sb[:], func=AF.Square), reads=[b_h], writes=[b_sq])
        rms_stats(None)
        for kc in range(KC):
            eng = 'dve'
            P.op(eng, lambda e, kc=kc: e.tensor_tensor(out=hn[:, kc, :], in0=h_sb[:, kc, :], in1=rstd[:], op=ALU.mult),
                 reads=[b_h, b_rstd], writes=[b_hn])
        for cb in range(11):
            wg, bwg = stream_w(w_fi, KC, cb * 256, 256, 1)
            wu, bwu = stream_w(w_fi, KC, DFF + cb * 256, 256, 1)
            for m2 in range(2):
                fc = cb * 2 + m2
                kg = nbank()
                for kc in range(KC):
                    P.op('pe', lambda e, kc=kc, k=kg, m2=m2, wt=wg: e.matmul(
                        bank[k][:, :], lhsT=wt[:, kc, m2 * 128:(m2 + 1) * 128], rhs=hn[:, kc, :],
                        start=(kc == 0), stop=(kc == KC - 1)), reads=[bwg, b_hn], writes=[b_bank[kg]])
                si = fc % 2
                P.op('act', lambda e, k=kg, si=si: e.activation(out=sg[si][:], in_=bank[k][:, :], func=AF.Silu),
                     reads=[b_bank[kg]], writes=[b_sg[si]])
                ku = nbank()
                for kc in range(KC):
                    P.op('pe', lambda e, kc=kc, k=ku, m2=m2, wt=wu: e.matmul(
                        bank[k][:, :], lhsT=wt[:, kc, m2 * 128:(m2 + 1) * 128], rhs=hn[:, kc, :],
                        start=(kc == 0), stop=(kc == KC - 1)), reads=[bwu, b_hn], writes=[b_bank[ku]])
                P.op('dve', lambda e, k=ku, si=si, fc=fc: e.tensor_tensor(out=act[:, fc, :], in0=bank[k][:, :],
                                                                          in1=sg[si][:], op=ALU.mult),
                     reads=[b_bank[ku], b_sg[si]], writes=[b_act])
        for cb in range(4):
            wt, bw = stream_w(w_fo, FC, cb * 256, 256, None)
            for m2 in range(2):
                mc = cb * 2 + m2
                k = nbank()
                for kc in range(FC):
                    P.op('pe', lambda e, kc=kc, k=k, m2=m2, wt=wt: e.matmul(
                        bank[k][:, :], lhsT=wt[:, kc, m2 * 128:(m2 + 1) * 128], rhs=act[:, kc, :],
                        start=(kc == 0), stop=(kc == FC - 1)), reads=[bw, b_act], writes=[b_bank[k]])
                P.op('act', lambda e, k=k, mc=mc: e.activation(out=mixed[:, mc, :], in_=bank[k][:, :], func=AF.Copy,
                                                               scale=g_sb[:, 2, mc:mc + 1]),
                     reads=[b_bank[k], b_g], writes=[b_mixed])
                P.op('act', lambda e, k=k, mc=mc: e.activation(out=sq[:, mc, :], in_=bank[k][:, :], func=AF.Square),
                     reads=[b_bank[k]], writes=[b_sq])
        rms_stats(None)
        post_norm_residual(mixed, b_mixed, 2, h_sb, b_h, r_sb, b_r)
        for half in range(2):
            P.dma('sp', oV[:, half * 4:(half + 1) * 4, tsl], r_sb[:, half * 4:(half + 1) * 4, :], reads=[b_r])


def build_phase_r(nc, P, stack, S, hT, w, g4, lbt, gnorm, mask2_in, rmask_in, ident_in, og):
    NT = S // 512
    sb = lambda name, shape, dt: stack.enter_context(nc.sbuf_tensor(name, shape, dt))
    ps = lambda name, shape, dt: stack.enter_context(nc.psum_tensor(name, shape, dt))
    B = Buf
    ones_bf = sb('ones_bf', [128, 128], BF16)
    eps_sb = sb('eps_sb', [128, 1], F32)
    g4_sb = sb('g4_sb', [128, 8], F32)
    gn_sb = sb('gn_sb', [128, 1], F32)
    lbt_sb = sb('lbt_sb', [128, 3, 2], F32)
    elb = sb('elb', [128, 3, 2], F32)
    s01 = sb('s01', [128, 2], F32)
    sall = sb('sall', [128, 2], F32)
    rs = sb('rs', [128, 2], F32)
    lb = sb('lb', [128, 2], F32)
    oml = sb('oml', [128, 2], F32)
    noml = sb('noml', [128, 2], F32)
    mask2_f = sb('mask2_f', [128, 128], F32)
    rmask = sb('rmask_sb', [128, 512], F32)
    ident_f = sb('ident_f', [128, 128], F32)
    ident = sb('ident_bf', [128, 128], BF16)
    wst = sb('wst', [128, KC, 512], F32)
    Wb = sb('Wb', [128, KC, 1024], BF16)
    sq = sb('sq', [128, KC, 512], BF16)
    hn = sb('hn', [128, KC, 512], BF16)
    lnr = sb('lnr', [128, 512], F32)
    rstd = sb('rstd', [128, 512], F32)
    q_sb = [sb('q_sb%d' % i, [128, 512], F32) for i in range(2)]
    sig = [sb('sig%d' % i, [128, 512], F32) for i in range(2)]
    logf = [sb('logf%d' % i, [128, 512], F32) for i in range(2)]
    kk = [sb('kk%d' % i, [128, 512], F32) for i in range(2)]
    bb = [sb('bb%d' % i, [128, 512], F32) for i in range(2)]
    eb = [sb('eb%d' % i, [128, 512], F32) for i in range(2)]
    enb = [sb('enb%d' % i, [128, 512], F32) for i in range(2)]
    qt = [sb('qt%d' % i, [128, 512], BF16) for i in range(2)]
    kt = [sb('kt%d' % i, [128, 512], BF16) for i in range(2)]
    sgl = [sb('sgl%d' % i, [128, 512], F32) for i in range(2)]
    v_tok = sb('v_tok', [128, 4, 256], BF16)
    kt_tok = sb('kt_tok', [128, 2, 4, 128], BF16)
    at_sb = [sb('at_sb%d' % i, [128, 128], BF16) for i in range(2)]
    T_sb = [sb('T_sb%d' % i, [128, 128], F32) for i in range(2)]
    St = [sb('St%d' % i, [128, 128], F32) for i in range(2)]
    Sbf = [[sb('Sbf%d_%d' % (i, v), [128, 128], BF16) for v in range(2)] for i in range(2)]
    o_sb = [sb('o_sb%d' % i, [128, 512], F32) for i in range(2)]
    osq = sb('osq', [128, 512], BF16)
    tmp = sb('tmp', [128, 512], F32)
    og_sb = [sb('og_sb%d' % i, [128, 512], BF16) for i in range(2)]
    bank = [ps('bank%d' % i, [128, 512], F32) for i in range(7)]
    bankT = ps('bankT', [128, 1024], BF16)

    b_const, b_g4, b_lb, b_mask2, b_rmask, b_ident, b_wst, b_Wb, b_sq, b_hn, b_lnr, b_rstd = [B() for _ in range(12)]
    b_q, b_sig, b_logf, b_kk, b_bb, b_eb, b_enb, b_qt, b_kt, b_sgl = [[B(), B()] for _ in range(10)]
    b_v, b_osq, b_tmp = B(), B(), B()
    b_kttok = [B(), B()]
    b_at = [B(), B()]
    b_T = [B(), B()]
    b_St = [B(), B()]
    b_Sbf = [[B(), B()], [B(), B()]]
    b_osb = [B(), B()]
    b_ogsb = [B(), B()]
    b_bank = [B() for _ in range(7)]
    b_bankT = B()

    P.op('pool', lambda e: e.memset(ones_bf[:], 1.0), writes=[b_const])
    P.op('pool', lambda e: e.memset(eps_sb[:], EPS), writes=[b_const])
    for i in range(2):
        P.op('pool', lambda e, i=i: e.memset(St[i][:], 0.0), writes=[b_St[i]])
        P.op('pool', lambda e, i=i: e.memset(Sbf[i][0][:], 0.0), writes=[b_Sbf[i][0]])
    P.dma('sp', g4_sb[:], g4[:, :], writes=[b_g4])
    P.dma('sp', gn_sb[:], gnorm[:, :], writes=[b_const])
    P.dma('sp', lbt_sb[:], lbt[:, :, :], writes=[b_lb])
    P.dma('sp', mask2_f[:], mask2_in[:, :], writes=[b_mask2])
    P.dma('sp', rmask[:], rmask_in[:, :], writes=[b_rmask])
    P.dma('sp', ident_f[:], ident_in[:, :], writes=[b_ident])
    P.op('dve', lambda e: e.tensor_copy(out=ident[:], in_=ident_f[:]), reads=[b_ident], writes=[b_ident])
    P.op('act', lambda e: e.activation(out=elb[:], in_=lbt_sb[:], func=AF.Exp), reads=[b_lb], writes=[b_lb])
    P.op('dve', lambda e: e.tensor_tensor(out=s01[:], in0=elb[:, 0, :], in1=elb[:, 1, :], op=ALU.add),
         reads=[b_lb], writes=[b_lb])
    P.op('dve', lambda e: e.tensor_tensor(out=sall[:], in0=s01[:], in1=elb[:, 2, :], op=ALU.add),
         reads=[b_lb], writes=[b_lb])
    P.op('dve', lambda e: e.reciprocal(out=rs[:], in_=sall[:]), reads=[b_lb], writes=[b_lb])
    P.op('dve', lambda e: e.tensor_tensor(out=lb[:], in0=s01[:], in1=rs[:], op=ALU.mult), reads=[b_lb], writes=[b_lb])
    P.op('dve', lambda e: e.tensor_tensor(out=oml[:], in0=elb[:, 2, :], in1=rs[:], op=ALU.mult),
         reads=[b_lb], writes=[b_lb])
    P.op('dve', lambda e: e.tensor_scalar(out=noml[:], in0=oml[:], scalar1=-1.0, scalar2=None, op0=ALU.mult),
         reads=[b_lb], writes=[b_lb])
    wv = w.rearrange("(kc p) n -> p kc n", p=128)
    for half in range(2):
        for kc in range(KC):
            P.dma('sp', wst[:, kc, :], wv[:, kc, half * 512:(half + 1) * 512], writes=[b_wst])
        for kc in range(KC):
            eng = 'dve'
            P.op(eng, lambda e, kc=kc, half=half: e.tensor_scalar(
                out=Wb[:, kc, half * 512:(half + 1) * 512], in0=wst[:, kc, :], scalar1=g4_sb[:, kc:kc + 1],
                scalar2=None, op0=ALU.mult), reads=[b_wst, b_g4], writes=[b_Wb])

    xv = hT.rearrange("(kc p) t -> p kc t", p=128)
    pbk = [0]

    def nbank():
        k = pbk[0] % 3
        pbk[0] += 1
        return k
    atc = [0]
    kvc = [0]
    ver = [0, 0]

    def do_tile(j):
        tsl = slice(j * 512, (j + 1) * 512)
        for half in range(2):
            P.dma('sp', wst[:, half * 4:(half + 1) * 4, :], xv[:, half * 4:(half + 1) * 4, tsl], writes=[b_wst])
        P.op('act', lambda e: e.activation(out=sq[:], in_=wst[:], func=AF.Square), reads=[b_wst], writes=[b_sq])
        k = nbank()
        for kc in range(KC):
            P.op('pe', lambda e, kc=kc, k=k: e.matmul(bank[k][:, :], lhsT=ones_bf[:], rhs=sq[:, kc, :],
                                                      start=(kc == 0), stop=(kc == KC - 1)),
                 reads=[b_const, b_sq], writes=[b_bank[k]])
        P.op('act', lambda e, k=k: e.activation(out=lnr[:], in_=bank[k][:, :], func=AF.Ln, bias=eps_sb[:, 0:1],
                                                scale=1.0 / D), reads=[b_bank[k], b_const], writes=[b_lnr])
        P.op('act', lambda e: e.activation(out=rstd[:], in_=lnr[:], func=AF.Exp, scale=-0.5),
             reads=[b_lnr], writes=[b_rstd])
        for kc in range(KC):
            eng = 'dve'
            P.op(eng, lambda e, kc=kc: e.tensor_tensor(out=hn[:, kc, :], in0=wst[:, kc, :], in1=rstd[:], op=ALU.mult),
                 reads=[b_wst, b_rstd], writes=[b_hn])

        def proj(col0):
            k = nbank()
            for kc in range(KC):
                P.op('pe', lambda e, kc=kc, k=k: e.matmul(bank[k][:, :], lhsT=Wb[:, kc, col0:col0 + 128],
                                                          rhs=hn[:, kc, :], start=(kc == 0), stop=(kc == KC - 1)),
                     reads=[b_Wb, b_hn], writes=[b_bank[k]])
            return k
        for hh in range(2):
            k = proj(hh * 128)
            P.op('act', lambda e, k=k, hh=hh: e.activation(out=q_sb[hh][:], in_=bank[k][:, :], func=AF.Copy),
                 reads=[b_bank[k]], writes=[b_q[hh]])
            k = proj(256 + hh * 128)
            P.op('act', lambda e, k=k, hh=hh: e.activation(out=sig[hh][:], in_=bank[k][:, :], func=AF.Sigmoid),
                 reads=[b_bank[k]], writes=[b_sig[hh]])
            P.op('act', lambda e, hh=hh: e.activation(out=logf[hh][:], in_=sig[hh][:], func=AF.Ln,
                                                      bias=lb[:, hh:hh + 1], scale=oml[:, hh:hh + 1]),
                 reads=[b_sig[hh], b_lb], writes=[b_logf[hh]])
            P.op('dve', lambda e, hh=hh: e.tensor_scalar(out=kk[hh][:], in0=sig[hh][:], scalar1=noml[:, hh:hh + 1],
                                                         scalar2=oml[:, hh:hh + 1], op0=ALU.mult, op1=ALU.add),
                 reads=[b_sig[hh], b_lb], writes=[b_kk[hh]])
            P.op('dve', lambda e, hh=hh: e.tensor_tensor_scan(out=bb[hh][:], data0=rmask[:], data1=logf[hh][:],
                                                              initial=0.0, op0=ALU.mult, op1=ALU.add),
                 reads=[b_rmask, b_logf[hh]], writes=[b_bb[hh]])
            P.op('act', lambda e, hh=hh: e.activation(out=eb[hh][:], in_=bb[hh][:], func=AF.Exp),
                 reads=[b_bb[hh]], writes=[b_eb[hh]])
            P.op('act', lambda e, hh=hh: e.activation(out=enb[hh][:], in_=bb[hh][:], func=AF.Exp, scale=-1.0),
                 reads=[b_bb[hh]], writes=[b_enb[hh]])
            P.op('dve', lambda e, hh=hh: e.tensor_tensor(out=qt[hh][:], in0=q_sb[hh][:], in1=eb[hh][:], op=ALU.mult),
                 reads=[b_q[hh], b_eb[hh]], writes=[b_qt[hh]])
            P.op('dve', lambda e, hh=hh: e.tensor_tensor(out=kt[hh][:], in0=kk[hh][:], in1=enb[hh][:], op=ALU.mult),
                 reads=[b_kk[hh], b_enb[hh]], writes=[b_kt[hh]])
            k = proj(768 + hh * 128)
            P.op('act', lambda e, k=k, hh=hh: e.activation(out=sgl[hh][:], in_=bank[k][:, :], func=AF.Silu),
                 reads=[b_bank[k]], writes=[b_sgl[hh]])
        for half in range(2):
            k = nbank()
            for s2 in range(2):
                sub = half * 2 + s2
                for kc in range(KC):
                    P.op('pe', lambda e, kc=kc, k=k, sub=sub, s2=s2: e.matmul(
                        bank[k][:, s2 * 256:(s2 + 1) * 256], lhsT=hn[:, kc, sub * 128:(sub + 1) * 128],
                        rhs=Wb[:, kc, 512:768], start=(kc == 0), stop=(kc == KC - 1)),
                        reads=[b_Wb, b_hn], writes=[b_bank[k]])
            P.op('dve', lambda e, k=k, half=half: e.tensor_copy(
                out=v_tok[:, half * 2:half * 2 + 2, :], in_=bank[k][:, :].rearrange("p (s n) -> p s n", s=2)),
                reads=[b_bank[k]], writes=[b_v])
        for hh in range(2):
            for sub in range(4):
                P.op('pe', lambda e, hh=hh, sub=sub: e.transpose(out=bankT[:, (hh * 4 + sub) * 128:(hh * 4 + sub + 1) * 128],
                                                                 in_=kt[hh][:, sub * 128:(sub + 1) * 128],
                                                                 identity=ident[:]),
                     reads=[b_kt[hh], b_ident], writes=[b_bankT])
            P.op('act', lambda e, hh=hh: e.activation(
                out=kt_tok[:, hh, :, :], in_=bankT[:, hh * 512:(hh + 1) * 512].rearrange("p (s n) -> p s n", s=4),
                func=AF.Copy), reads=[b_bankT], writes=[b_kttok[hh]])
        def do_sub(sub):
            csl = slice(sub * 128, (sub + 1) * 128)
            ai = {}
            for hh in range(2):
                a = atc[0] % 4
                atc[0] += 1
                ai[hh] = a
                P.op('pe', lambda e, hh=hh, a=a: e.matmul(bank[3][:, a * 128:(a + 1) * 128], lhsT=kt[hh][:, csl],
                                                          rhs=qt[hh][:, csl], start=True, stop=True),
                     reads=[b_kt[hh], b_qt[hh]], writes=[b_bank[3]])
            for hh in range(2):
                a = ai[hh]
                P.op('dve', lambda e, hh=hh, a=a: e.tensor_tensor(out=at_sb[hh][:], in0=bank[3][:, a * 128:(a + 1) * 128],
                                                                  in1=mask2_f[:], op=ALU.mult),
                     reads=[b_bank[3], b_mask2], writes=[b_at[hh]])
            for hh in range(2):
                P.op('pe', lambda e, hh=hh: e.matmul(bank[4 + hh][:, csl], lhsT=v_tok[:, sub, hh * 128:(hh + 1) * 128],
                                                     rhs=at_sb[hh][:], start=True, stop=False),
                     reads=[b_v, b_at[hh]], writes=[b_bank[4 + hh]])
            def do_c(c):
                ki = {}
                for hh in range(2):
                    vv = ver[hh]
                    c0 = sub * 128 + c * 64
                    P.op('pe', lambda e, hh=hh, vv=vv, c0=c0, c=c: e.matmul(
                        bank[4 + hh][:, c0:c0 + 64], lhsT=Sbf[hh][vv][:], rhs=qt[hh][:, c0:c0 + 64],
                        start=False, stop=(c == 1)), reads=[b_Sbf[hh][vv], b_qt[hh]], writes=[b_bank[4 + hh]])
                    ks = kvc[0] % 4
                    kvc[0] += 1
                    ki[hh] = ks
                    P.op('pe', lambda e, hh=hh, ks=ks, c=c: e.matmul(
                        bank[6][:, ks * 128:(ks + 1) * 128], lhsT=kt_tok[c * 64:(c + 1) * 64, hh, sub, :],
                        rhs=v_tok[c * 64:(c + 1) * 64, sub, hh * 128:(hh + 1) * 128], start=True, stop=True),
                        reads=[b_kttok[hh], b_v], writes=[b_bank[6]])
                for hh in range(2):
                    ks = ki[hh]
                    c0 = sub * 128 + c * 64
                    ecol = c0 + 63
                    P.op('act', lambda e, hh=hh, ks=ks, ecol=ecol: e.activation(
                        out=T_sb[hh][:], in_=bank[6][:, ks * 128:(ks + 1) * 128], func=AF.Copy,
                        scale=eb[hh][:, ecol:ecol + 1]), reads=[b_bank[6], b_eb[hh]], writes=[b_T[hh]])
                    P.op('dve', lambda e, hh=hh, ecol=ecol: e.scalar_tensor_tensor(
                        out=St[hh][:], in0=St[hh][:], scalar=eb[hh][:, ecol:ecol + 1], in1=T_sb[hh][:],
                        op0=ALU.mult, op1=ALU.add), reads=[b_St[hh], b_eb[hh], b_T[hh]], writes=[b_St[hh]])
                    nv = 1 - ver[hh]
                    P.op('dve', lambda e, hh=hh, nv=nv: e.tensor_copy(out=Sbf[hh][nv][:], in_=St[hh][:]),
                         reads=[b_St[hh]], writes=[b_Sbf[hh][nv]])
                    ver[hh] = nv
            do_c(0)
            do_c(1)
            for hh in range(2):
                P.op('act', lambda e, hh=hh: e.activation(out=o_sb[hh][:, csl], in_=bank[4 + hh][:, csl], func=AF.Copy),
                     reads=[b_bank[4 + hh]], writes=[b_osb[hh]])
        for sub in range(4):
            do_sub(sub)
        for hh in range(2):
            P.op('act', lambda e, hh=hh: e.activation(out=osq[:], in_=o_sb[hh][:], func=AF.Square),
                 reads=[b_osb[hh]], writes=[b_osq])
            k = nbank()
            P.op('pe', lambda e, k=k: e.matmul(bank[k][:, :], lhsT=ones_bf[:], rhs=osq[:], start=True, stop=True),
                 reads=[b_const, b_osq], writes=[b_bank[k]])
            P.op('act', lambda e, k=k: e.activation(out=lnr[:], in_=bank[k][:, :], func=AF.Ln, bias=eps_sb[:, 0:1],
                                                    scale=1.0 / 128), reads=[b_bank[k], b_const], writes=[b_lnr])
            P.op('act', lambda e: e.activation(out=rstd[:], in_=lnr[:], func=AF.Exp, scale=-0.5),
                 reads=[b_lnr], writes=[b_rstd])
            P.op('dve', lambda e, hh=hh: e.tensor_tensor(out=tmp[:], in0=o_sb[hh][:], in1=rstd[:], op=ALU.mult),
                 reads=[b_osb[hh], b_rstd], writes=[b_tmp])
            P.op('dve', lambda e, hh=hh: e.scalar_tensor_tensor(out=og_sb[hh][:], in0=tmp[:], scalar=gn_sb[:, 0:1],
                                                                in1=sgl[hh][:], op0=ALU.mult, op1=ALU.mult),
                 reads=[b_tmp, b_const, b_sgl[hh]], writes=[b_ogsb[hh]])
            P.dma('sp', og[hh * 128:(hh + 1) * 128, tsl], og_sb[hh][:], reads=[b_ogsb[hh]])

    for j in range(NT):
        do_tile(j)


SEQ = 8192
_CACHE = {}


def _build_pa():
    nc = bass.Bass("TRN2", target_bir_lowering=False)
    xT = nc.dram_tensor("xT", [1024, SEQ], F32, kind="ExternalInput").ap()
    w = nc.dram_tensor("w", [1024, 772], F32, kind="ExternalInput").ap()
    g0 = nc.dram_tensor("g0", [128, 8], F32, kind="ExternalInput").ap()
    bf = nc.dram_tensor("bf", [4, 1], F32, kind="ExternalInput").ap()
    tri_in = nc.dram_tensor("tri", [128, 128], F32, kind="ExternalInput").ap()
    out = nc.dram_tensor("attn", [256, SEQ], BF16, kind="ExternalOutput").ap()
    with ExitStack() as gs:
        P = Prog(nc, gs)
        with ExitStack() as st:
            build_phase_a(nc, P, st, SEQ, xT, w, g0, bf, tri_in, out)
            P.emit()
    return nc


def _build_pd():
    T = 2048
    nc = bass.Bass("TRN2", target_bir_lowering=False)
    aT = nc.dram_tensor("aT", [1024, T], BF16, kind="ExternalInput").ap()
    rT = nc.dram_tensor("rT", [1024, T], F32, kind="ExternalInput").ap()
    wo = nc.dram_tensor("wo", [1024, 1024], F32, kind="ExternalInput").ap()
    wfi = nc.dram_tensor("wfi", [1024, 5632], F32, kind="ExternalInput").ap()
    wfo = nc.dram_tensor("wfo", [2816, 1024], F32, kind="ExternalInput").ap()
    gains = nc.dram_tensor("gains", [128, 3, 8], F32, kind="ExternalInput").ap()
    out = nc.dram_tensor("outT", [1024, T], F32, kind="ExternalOutput").ap()
    with ExitStack() as gs:
        P = Prog(nc, gs)
        with ExitStack() as st:
            build_phase_d(nc, P, st, T, aT, rT, wo, wfi, wfo, gains, out)
            P.emit()
    return nc


def _build_pr():
    nc = bass.Bass("TRN2", target_bir_lowering=False)
    hT = nc.dram_tensor("hT", [1024, SEQ], F32, kind="ExternalInput").ap()
    w = nc.dram_tensor("w", [1024, 1024], F32, kind="ExternalInput").ap()
    g4d = nc.dram_tensor("g4", [128, 8], F32, kind="ExternalInput").ap()
    lbtd = nc.dram_tensor("lbt", [128, 3, 2], F32, kind="ExternalInput").ap()
    gnd = nc.dram_tensor("gn", [128, 1], F32, kind="ExternalInput").ap()
    m2 = nc.dram_tensor("mask2", [128, 128], F32, kind="ExternalInput").ap()
    rm = nc.dram_tensor("rmask", [128, 512], F32, kind="ExternalInput").ap()
    idd = nc.dram_tensor("ident", [128, 128], F32, kind="ExternalInput").ap()
    og = nc.dram_tensor("og", [256, SEQ], BF16, kind="ExternalOutput").ap()
    with ExitStack() as gs:
        P = Prog(nc, gs)
        with ExitStack() as st:
            build_phase_r(nc, P, st, SEQ, hT, w, g4d, lbtd, gnd, m2, rm, idd, og)
            P.emit()
    return nc


def _get(name, fn):
    if name not in _CACHE:
        _CACHE[name] = fn()
    return _CACHE[name]


def _pk(g):
    return np.ascontiguousarray(np.asarray(g, np.float32).reshape(8, 128).T)


def kernel(x, fox_w_in, fox_b_f, fox_w_out, hgrn_w_in, hgrn_lb_table, hgrn_gnorm, hgrn_w_out, ffn_w_in,
           ffn_w_out, norm_gains):
    x = np.asarray(x, np.float32)
    fox_w_in = np.asarray(fox_w_in, np.float32)
    hgrn_w_in = np.asarray(hgrn_w_in, np.float32)
    norm_gains = np.asarray(norm_gains, np.float32)
    ffn_w_in = np.asarray(ffn_w_in, np.float32)
    ffn_w_out = np.asarray(ffn_w_out, np.float32)
    cores = list(range(8))
    xT = [np.ascontiguousarray(x[b].T) for b in range(2)]
    tri = (np.arange(128)[None, :] >= np.arange(128)[:, None]).astype(np.float32)
    ii = np.arange(128)
    mask2 = ((ii[:, None] // 64 == ii[None, :] // 64) & (ii[None, :] >= ii[:, None])).astype(np.float32)
    rmask = np.ascontiguousarray((np.arange(512) % 64 != 0).astype(np.float32)[None, :].repeat(128, 0))
    ident = np.eye(128, dtype=np.float32)

    in_maps = []
    for c in cores:
        b, g = c // 4, c % 4
        heads = [4 * g + i for i in range(4)]
        cols = lambda base: np.concatenate([np.arange(base + h * 64, base + (h + 1) * 64) for h in heads])
        wsel = np.concatenate([fox_w_in[0][:, cols(0)], fox_w_in[0][:, cols(1024)], fox_w_in[0][:, cols(2048)],
                               fox_w_in[0][:, 3072 + np.array(heads)]], axis=1)
        in_maps.append({"xT": xT[b], "w": np.ascontiguousarray(wsel), "g0": _pk(norm_gains[0, 0]),
                        "bf": np.ascontiguousarray(np.asarray(fox_b_f, np.float32)[0, heads].reshape(4, 1)),
                        "tri": tri})
    res = run_bass_kernel_spmd(_get('pa', _build_pa), in_maps, core_ids=cores)
    attnT = [np.concatenate([np.asarray(res.results[b * 4 + g]["attn"]) for g in range(4)], axis=0)
             for b in range(2)]

    def dense(aT, rT, w_o, w_fi, w_fo, g3):
        gains = np.ascontiguousarray(np.asarray(g3, np.float32).reshape(3, 8, 128).transpose(2, 0, 1))
        in_maps = []
        for c in cores:
            b, r = c // 4, c % 4
            sl = slice(r * 2048, (r + 1) * 2048)
            in_maps.append({"aT": np.ascontiguousarray(aT[b][:, sl]), "rT": np.ascontiguousarray(rT[b][:, sl]),
                            "wo": np.ascontiguousarray(w_o), "wfi": np.ascontiguousarray(w_fi),
                            "wfo": np.ascontiguousarray(w_fo), "gains": gains})
        res = run_bass_kernel_spmd(_get('pd', _build_pd), in_maps, core_ids=cores)
        return [np.concatenate([np.asarray(res.results[b * 4 + r]["outT"]) for r in range(4)], axis=1)
                for b in range(2)]

    h1T = dense(attnT, xT, np.asarray(fox_w_out, np.float32)[0], ffn_w_in[0], ffn_w_out[0], norm_gains[0, 1:4])

    lbtab = np.asarray(hgrn_lb_table, np.float32)
    in_maps = []
    for c in cores:
        b, hp = c // 4, c % 4
        heads = [2 * hp, 2 * hp + 1]
        cols = lambda base: np.concatenate([np.arange(base + h * 128, base + (h + 1) * 128) for h in heads])
        wsel = np.concatenate([hgrn_w_in[0][:, cols(0)], hgrn_w_in[0][:, cols(1024)], hgrn_w_in[0][:, cols(2048)],
                               hgrn_w_in[0][:, cols(3072)]], axis=1)
        lbt = np.stack([lbtab[:, h * 128:(h + 1) * 128] for h in heads], axis=-1)
        in_maps.append({"hT": h1T[b], "w": np.ascontiguousarray(wsel), "g4": _pk(norm_gains[1, 0]),
                        "lbt": np.ascontiguousarray(lbt.transpose(1, 0, 2)),
                        "gn": np.ascontiguousarray(np.asarray(hgrn_gnorm, np.float32)[0].reshape(128, 1)),
                        "mask2": mask2, "rmask": rmask, "ident": ident})
    res = run_bass_kernel_spmd(_get('pr', _build_pr), in_maps, core_ids=cores)
    ogT = [np.concatenate([np.asarray(res.results[b * 4 + hp]["og"]) for hp in range(4)], axis=0)
           for b in range(2)]

    outT = dense(ogT, h1T, np.asarray(hgrn_w_out, np.float32)[0], ffn_w_in[1], ffn_w_out[1], norm_gains[1, 1:4])
    return np.ascontiguousarray(np.stack([outT[b].T for b in range(2)], axis=0)).astype(np.float32)
```
